# Optimizing a Trainium2 kernel written in Bass

```python
import jax, jax.numpy as jnp
from jax import lax
import numpy as np

D_MODEL = 1024
BATCH = 16
SEQ = 4096
DEPTH = 2

N_EVEN = (DEPTH + 1) // 2
N_ODD = DEPTH // 2
D_FF = 4 * D_MODEL
NORM_EPS = 1e-6
CHUNK = 128

RWKV_WIDTH = D_MODEL // 2
RWKV_HEAD_DIM = 64
RWKV_HEADS = RWKV_WIDTH // RWKV_HEAD_DIM
DECAY_LORA = 64
ICL_LORA = 64
GATE_LORA = 128
RWKV_GN_EPS = 64e-5
RWKV_COLS = 3 * RWKV_WIDTH + DECAY_LORA + ICL_LORA + GATE_LORA

RET_WIDTH = D_MODEL - RWKV_WIDTH
RET_HEADS = 4
RET_HEAD_DIM = RET_WIDTH // RET_HEADS
ROPE_BASE = 10000.0
RET_COLS = 4 * RET_WIDTH
AB_COLS = RWKV_COLS + RET_COLS

MLSTM_HEADS = 8
MLSTM_QK_DIM = D_MODEL // 2 // MLSTM_HEADS
MLSTM_V_DIM = D_MODEL // MLSTM_HEADS
MLSTM_CONV = 4
GATE_SOFTCAP = 15.0
MLSTM_QK_COLS = 2 * MLSTM_HEADS * MLSTM_QK_DIM
C_COLS = MLSTM_QK_COLS + 2 * D_MODEL + 2 * MLSTM_HEADS

kernel_name = 'hybrid_rwkv7_retention_mlstm_trunk'

F32 = jnp.float32


def rms_norm(x, g):
    xf = x.astype(F32)
    y = xf * lax.rsqrt(jnp.mean(xf * xf, axis=-1, keepdims=True) + NORM_EPS)
    return (y * g.astype(F32)).astype(x.dtype)


def token_shift(x):
    return jnp.pad(x, ((0, 0), (1, 0), (0, 0)))[:, :-1]


def split_heads(x, n_heads):
    return x.reshape(x.shape[:-1] + (n_heads, x.shape[-1] // n_heads))


def rwkv7_scan(r, decay, k, v, a, b):
    bsz, _, n_heads, n = r.shape
    xs = tuple(jnp.moveaxis(t, 1, 0) for t in (r, decay, k, v, a, b))

    def step(state, inp):
        r_t, w_t, k_t, v_t, a_t, b_t = inp
        sa = jnp.einsum('bhij,bhj->bhi', state, a_t)
        state = (state * w_t[:, :, None, :] + sa[..., None] * b_t[:, :, None, :]
                 + v_t[..., None] * k_t[:, :, None, :])
        return state, jnp.einsum('bhij,bhj->bhi', state, r_t)

    s0 = jnp.zeros((bsz, n_heads, n, n), F32)
    _, y = lax.scan(step, s0, xs)
    return jnp.moveaxis(y, 0, 1)


def rwkv7_group(p, mu, w0, w_up, a0, a_up, g_up, k_k, k_a, r_k, ln_w, ln_b):
    bsz, seq, _ = p.shape
    p = p + (token_shift(p) - p) * mu
    cuts = [RWKV_WIDTH, 2 * RWKV_WIDTH, 3 * RWKV_WIDTH,
            3 * RWKV_WIDTH + DECAY_LORA, 3 * RWKV_WIDTH + DECAY_LORA + ICL_LORA]
    r, k, v, w_lo, a_lo, g_lo = jnp.split(p, cuts, axis=-1)
    w = -jax.nn.softplus(-(w0 + jnp.tanh(w_lo) @ w_up)) - 0.5
    decay = jnp.exp(-jnp.exp(w.astype(F32)))
    a = jax.nn.sigmoid(a0 + a_lo @ a_up)
    g = jax.nn.sigmoid(g_lo) @ g_up
    kk = split_heads((k * k_k).astype(F32), RWKV_HEADS)
    kk = kk * lax.rsqrt(jnp.maximum(jnp.sum(kk * kk, -1, keepdims=True), 1e-24))
    k = k * (1.0 + (a - 1.0) * k_a)
    rh, kh, vh, ah = [split_heads(t.astype(F32), RWKV_HEADS) for t in (r, k, v, a)]
    dh = split_heads(decay, RWKV_HEADS)
    y = rwkv7_scan(rh, dh, kh, vh, -kk, kk * ah)
    mean = jnp.mean(y, -1, keepdims=True)
    var = jnp.mean(jnp.square(y - mean), -1, keepdims=True)
    y = (y - mean) * lax.rsqrt(var + RWKV_GN_EPS)
    y = y.reshape(bsz, seq, RWKV_WIDTH) * ln_w + ln_b
    bonus = jnp.sum(rh * kh * r_k, -1, keepdims=True) * vh
    y = (y + bonus.reshape(bsz, seq, RWKV_WIDTH)) * g
    return y.astype(p.dtype)


def rotary(x, pos):
    d = x.shape[-1]
    inv = ROPE_BASE ** (-jnp.arange(0, d, 2, dtype=F32) / d)
    ang = pos[:, None] * inv[None, :]
    cos = jnp.cos(ang)[None, :, None, :]
    sin = jnp.sin(ang)[None, :, None, :]
    x1, x2 = x[..., : d // 2], x[..., d // 2:]
    return jnp.concatenate([x1 * cos - x2 * sin, x1 * sin + x2 * cos], axis=-1)


def chunk_bthd(x):
    bsz, seq, h, d = x.shape
    return x.reshape(bsz, seq // CHUNK, CHUNK, h, d).transpose(1, 0, 3, 2, 4)


def unchunk_bthd(x):
    nc, bsz, h, l, d = x.shape
    return x.transpose(1, 0, 3, 2, 4).reshape(bsz, nc * l, h, d)


def retention_chunkwise(q, k, v):
    bsz, _, n_heads, d = q.shape
    log_gamma = jnp.log1p(-jnp.exp2(-5.0 - jnp.arange(n_heads, dtype=F32)))
    idx = jnp.arange(CHUNK, dtype=F32)
    rel = idx[:, None] - idx[None, :]
    causal = rel >= 0
    decay_mat = jnp.where(causal[None], jnp.exp(log_gamma[:, None, None] * jnp.where(causal, rel, 0.0)[None]), 0.0)
    q_decay = jnp.exp(log_gamma[:, None] * (idx + 1.0)[None])
    k_decay = jnp.exp(log_gamma[:, None] * (CHUNK - 1.0 - idx)[None])
    chunk_decay = jnp.exp(log_gamma * CHUNK)

    def step(state, inp):
        qc, kc, vc = inp
        s = jnp.einsum('bhld,bhmd->bhlm', qc, kc) * decay_mat
        o = (jnp.einsum('bhlm,bhme->bhle', s, vc)
             + jnp.einsum('bhld,bhde->bhle', qc, state) * q_decay[None, :, :, None])
        state = (state * chunk_decay[None, :, None, None]
                 + jnp.einsum('bhmd,bhme->bhde', kc * k_decay[None, :, :, None], vc))
        return state, o

    s0 = jnp.zeros((bsz, n_heads, d, d), F32)
    _, o = lax.scan(step, s0, (chunk_bthd(q), chunk_bthd(k), chunk_bthd(v)))
    return unchunk_bthd(o)


def retention_group(p, pos):
    bsz, seq, _ = p.shape
    q, k, v, g = jnp.split(p, 4, axis=-1)
    q = rotary(split_heads(q.astype(F32), RET_HEADS), pos)
    k = rotary(split_heads(k.astype(F32), RET_HEADS), pos) * (RET_HEAD_DIM ** -0.5)
    v = split_heads(v.astype(F32), RET_HEADS)
    o = retention_chunkwise(q, k, v)
    o = o * lax.rsqrt(jnp.mean(o * o, -1, keepdims=True) + NORM_EPS)
    y = o.reshape(bsz, seq, RET_WIDTH) * jax.nn.silu(g.astype(F32))
    return y.astype(p.dtype)


def causal_depthwise_conv(x, w, b):
    n_taps = w.shape[0]
    seq = x.shape[1]
    xp = jnp.pad(x, ((0, 0), (n_taps - 1, 0), (0, 0)))
    return sum(xp[:, j:j + seq] * w[j] for j in range(n_taps)) + b


def to_chunks(x):
    bsz, h, seq = x.shape[:3]
    x = x.reshape((bsz, h, seq // CHUNK, CHUNK) + x.shape[3:])
    return jnp.moveaxis(x, 2, 0)


def from_chunks(x):
    x = jnp.moveaxis(x, 0, 2)
    return x.reshape(x.shape[:2] + (x.shape[2] * x.shape[3],) + x.shape[4:])


def mlstm_chunkwise(q, k, v, log_i, log_f):
    bsz, n_heads, _, dk = q.shape
    dv = v.shape[-1]
    causal = jnp.tril(jnp.ones((CHUNK, CHUNK), bool))

    def step(carry, inp):
        c_st, n_st, m_st = carry
        qc, kc, vc, li, lf = inp
        b = jnp.cumsum(lf, axis=-1)
        log_d = jnp.where(causal, b[..., :, None] - b[..., None, :] + li[..., None, :], -jnp.inf)
        log_inter = b + m_st[..., None]
        m_t = jnp.maximum(jnp.max(log_d, axis=-1), log_inter)
        s = jnp.einsum('bhld,bhsd->bhls', qc, kc) * jnp.exp(log_d - m_t[..., None])
        inter = jnp.exp(log_inter - m_t)
        num = (jnp.einsum('bhls,bhse->bhle', s, vc)
               + inter[..., None] * jnp.einsum('bhld,bhde->bhle', qc, c_st))
        den = jnp.sum(s, axis=-1) + inter * jnp.einsum('bhld,bhd->bhl', qc, n_st)
        h = num / jnp.maximum(jnp.abs(den), jnp.exp(-m_t))[..., None]
        b_end = b[..., -1]
        log_w = b_end[..., None] - b + li
        m_new = jnp.maximum(b_end + m_st, jnp.max(log_w, axis=-1))
        kw = kc * jnp.exp(log_w - m_new[..., None])[..., None]
        carry_scale = jnp.exp(b_end + m_st - m_new)
        c_st = carry_scale[..., None, None] * c_st + jnp.einsum('bhsd,bhse->bhde', kw, vc)
        n_st = carry_scale[..., None] * n_st + jnp.sum(kw, axis=2)
        return (c_st, n_st, m_new), h

    carry0 = (jnp.zeros((bsz, n_heads, dk, dv), F32),
              jnp.zeros((bsz, n_heads, dk), F32),
              jnp.zeros((bsz, n_heads), F32))
    xs = (to_chunks(q), to_chunks(k), to_chunks(v), to_chunks(log_i), to_chunks(log_f))
    _, h = lax.scan(step, carry0, xs)
    return from_chunks(h)


def softcap(x):
    return GATE_SOFTCAP * jnp.tanh(x / GATE_SOFTCAP)


def mlstm_group(p, conv_w, conv_b, i_bias, f_bias, norm_w):
    bsz, seq, _ = p.shape
    cuts = [MLSTM_QK_COLS, MLSTM_QK_COLS + D_MODEL, MLSTM_QK_COLS + 2 * D_MODEL,
            MLSTM_QK_COLS + 2 * D_MODEL + MLSTM_HEADS]
    qk, v, o, i_pre, f_pre = jnp.split(p, cuts, axis=-1)
    qk = jax.nn.silu(causal_depthwise_conv(qk, conv_w, conv_b))
    q, k = jnp.split(qk.astype(F32), 2, axis=-1)
    q = split_heads(q, MLSTM_HEADS).transpose(0, 2, 1, 3)
    k = split_heads(k, MLSTM_HEADS).transpose(0, 2, 1, 3) * (MLSTM_QK_DIM ** -0.5)
    vh = split_heads(v.astype(F32), MLSTM_HEADS).transpose(0, 2, 1, 3)
    log_i = softcap((i_pre + i_bias).astype(F32)).transpose(0, 2, 1)
    log_f = jax.nn.log_sigmoid(softcap((f_pre + f_bias).astype(F32))).transpose(0, 2, 1)
    h = mlstm_chunkwise(q, k, vh, log_i, log_f).transpose(0, 2, 1, 3)
    h = h * lax.rsqrt(jnp.mean(h * h, -1, keepdims=True) + NORM_EPS)
    h = h * norm_w.astype(F32).reshape(MLSTM_HEADS, MLSTM_V_DIM)
    y = h.reshape(bsz, seq, D_MODEL) * jax.nn.sigmoid(o.astype(F32))
    return y.astype(p.dtype)


def squared_relu_mlp(h, w1, w2):
    return jnp.square(jax.nn.relu(h @ w1)) @ w2


def setup_inputs(seed: int = 0) -> dict:
    key = jax.random.key(seed)
    ks = jax.random.split(key, 32)
    nrm = jax.random.normal
    uni = jax.random.uniform
    E, O, D = N_EVEN, N_ODD, D_MODEL
    return {
        'x': nrm(ks[0], (BATCH, SEQ, D), F32),
        'norm_mix_g': 1.0 + 0.02 * nrm(ks[1], (DEPTH, D), F32),
        'norm_mlp_g': 1.0 + 0.02 * nrm(ks[2], (DEPTH, D), F32),
        'norm_final_g': 1.0 + 0.02 * nrm(ks[3], (D,), F32),
        'ab_w_in': nrm(ks[4], (E, D, AB_COLS), F32) * D ** -0.5,
        'rwkv_mu': uni(ks[5], (E, RWKV_COLS), F32),
        'rwkv_w0': uni(ks[6], (E, RWKV_WIDTH), F32, minval=-6.5, maxval=-1.5),
        'rwkv_w_up': nrm(ks[7], (E, DECAY_LORA, RWKV_WIDTH), F32) * 0.1 * DECAY_LORA ** -0.5,
        'rwkv_a0': 0.1 * nrm(ks[8], (E, RWKV_WIDTH), F32),
        'rwkv_a_up': nrm(ks[9], (E, ICL_LORA, RWKV_WIDTH), F32) * ICL_LORA ** -0.5,
        'rwkv_g_up': nrm(ks[10], (E, GATE_LORA, RWKV_WIDTH), F32) * GATE_LORA ** -0.5,
        'rwkv_k_k': 0.85 + 0.05 * nrm(ks[11], (E, RWKV_WIDTH), F32),
        'rwkv_k_a': 1.0 + 0.05 * nrm(ks[12], (E, RWKV_WIDTH), F32),
        'rwkv_r_k': 0.1 * nrm(ks[13], (E, RWKV_HEADS, RWKV_HEAD_DIM), F32),
        'rwkv_ln_w': 1.0 + 0.02 * nrm(ks[14], (E, RWKV_WIDTH), F32),
        'rwkv_ln_b': 0.02 * nrm(ks[15], (E, RWKV_WIDTH), F32),
        'ab_w_out': nrm(ks[16], (E, D, D), F32) * D ** -0.5,
        'c_w_in': nrm(ks[17], (O, D, C_COLS), F32) * D ** -0.5,
        'c_conv_w': nrm(ks[18], (O, MLSTM_CONV, MLSTM_QK_COLS), F32) * MLSTM_CONV ** -0.5,
        'c_conv_b': 0.02 * nrm(ks[19], (O, MLSTM_QK_COLS), F32),
        'c_i_bias': 0.1 * nrm(ks[20], (O, MLSTM_HEADS), F32),
        'c_f_bias': uni(ks[21], (O, MLSTM_HEADS), F32, minval=3.0, maxval=6.0),
        'c_norm_w': 1.0 + 0.02 * nrm(ks[22], (O, D), F32),
        'c_w_out': nrm(ks[23], (O, D, D), F32) * D ** -0.5,
        'mlp_w1': nrm(ks[24], (DEPTH, D, D_FF), F32) * D ** -0.5,
        'mlp_w2': nrm(ks[25], (DEPTH, D_FF, D), F32) * D_FF ** -0.5,
    }


def reference(x, norm_mix_g, norm_mlp_g, norm_final_g, ab_w_in, rwkv_mu, rwkv_w0,
              rwkv_w_up, rwkv_a0, rwkv_a_up, rwkv_g_up, rwkv_k_k, rwkv_k_a, rwkv_r_k,
              rwkv_ln_w, rwkv_ln_b, ab_w_out, c_w_in, c_conv_w, c_conv_b, c_i_bias,
              c_f_bias, c_norm_w, c_w_out, mlp_w1, mlp_w2):
    seq = x.shape[1]
    pos = jnp.arange(seq, dtype=F32)
    for layer in range(DEPTH):
        h = rms_norm(x, norm_mix_g[layer])
        j = layer // 2
        if layer % 2 == 0:
            p = h @ ab_w_in[j]
            y_a = rwkv7_group(p[..., :RWKV_COLS], rwkv_mu[j], rwkv_w0[j], rwkv_w_up[j],
                              rwkv_a0[j], rwkv_a_up[j], rwkv_g_up[j], rwkv_k_k[j],
                              rwkv_k_a[j], rwkv_r_k[j], rwkv_ln_w[j], rwkv_ln_b[j])
            y_b = retention_group(p[..., RWKV_COLS:], pos)
            y = jnp.concatenate([y_a, y_b], axis=-1) @ ab_w_out[j]
        else:
            p = h @ c_w_in[j]
            y = mlstm_group(p, c_conv_w[j], c_conv_b[j], c_i_bias[j], c_f_bias[j],
                            c_norm_w[j]) @ c_w_out[j]
        x = x + y.astype(x.dtype)
        h = rms_norm(x, norm_mlp_g[layer])
        x = x + squared_relu_mlp(h, mlp_w1[layer], mlp_w2[layer]).astype(x.dtype)
    return rms_norm(x, norm_final_g)
```

```python
import contextlib
import math
import numpy as np
import ml_dtypes
import concourse.bass as bass
import concourse.mybir as mybir
from concourse.bass_utils import run_bass_kernel_spmd

F32 = mybir.dt.float32
BF16 = mybir.dt.bfloat16
AF = mybir.ActivationFunctionType
ALU = mybir.AluOpType
AX = mybir.AxisListType

ENGS = ("pe", "act", "dve", "pool", "sp")
NCORES = 8
D = 1024
G = 512
KAPPA = math.exp(-0.5)
NSLAB = 52
NSLOT = 3


class Prog:
    def __init__(self, nc):
        self.nc = nc
        self.q = {e: [] for e in ENGS}
        self.lastw = {}
        self.readers = {}
        self.seen = {e: {} for e in ENGS}
        self.ord = 0

    def op(self, eng, fn, reads=(), writes=(), dma=False):
        q = self.q[eng]
        idx = len(q)
        deps = set()
        for k in reads:
            w = self.lastw.get(k)
            if w is not None:
                deps.add(w)
            if k.startswith("ps"):
                for r in self.readers.get(k, ()):
                    if r[0] != eng:
                        deps.add(r)
        for k in writes:
            w = self.lastw.get(k)
            if w is not None:
                deps.add(w)
            for r in self.readers.get(k, ()):
                deps.add(r)
        waits = []
        seen = self.seen[eng]
        best = {}
        cands = []
        for (e2, i2) in deps:
            isdma = self.q[e2][i2]["dma"]
            if e2 == eng and eng == "pe" and not isdma:
                continue
            if isdma:
                cands.append((e2, i2))
            else:
                best[e2] = max(best.get(e2, -1), i2)
        cands += list(best.items())
        cands.sort(key=lambda d: -self.q[d[0]][d[1]]["ord"])
        for (e2, i2) in cands:
            ent = self.q[e2][i2]
            if ent["dma"]:
                if (e2, i2) in seen:
                    continue
                seen[(e2, i2)] = 1
            else:
                if seen.get(e2, -1) >= i2:
                    continue
                seen[e2] = i2
                for x, v in ent["snap"].items():
                    if isinstance(x, tuple):
                        seen[x] = 1
                    elif seen.get(x, -1) < v:
                        seen[x] = v
            ent["sig"] = True
            waits.append((e2, i2))
        self.ord += 1
        snap = dict(seen)
        snap[eng] = idx if not dma else snap.get(eng, -1)
        q.append(dict(fn=fn, waits=waits, sig=False, dma=dma, snap=snap, ord=self.ord))
        me = (eng, idx)
        for k in reads:
            self.readers.setdefault(k, []).append(me)
        for k in writes:
            self.lastw[k] = me
            self.readers[k] = []
        return me

    def require(self, eng, dep):
        self.q[eng].append(dict(fn=None, waits=[dep], sig=False, dma=False, snap={}, ord=self.ord))
        self.q[dep[0]][dep[1]]["sig"] = True

    def emit(self):
        nc = self.nc
        with contextlib.ExitStack() as st:
            sems = {e: st.enter_context(nc.semaphore("s_" + e)) for e in ENGS}
            dsems, dcnt, dassign = {}, {}, {}
            for e in ENGS:
                for i, it in enumerate(self.q[e]):
                    if it["dma"] and it["sig"]:
                        nm = it["dma"]
                        if nm not in dsems:
                            dsems[nm] = st.enter_context(nc.semaphore("d_" + nm))
                            dcnt[nm] = 0
                        dcnt[nm] += 16
                        dassign[(e, i)] = (dsems[nm], dcnt[nm])
            cnt = {}
            for e in ENGS:
                c = 0
                for i, it in enumerate(self.q[e]):
                    if it["sig"] and not it["dma"]:
                        c += 1
                        cnt[(e, i)] = c
            block = st.enter_context(nc.Block())

            def run(engname, eng):
                for i, it in enumerate(self.q[engname]):
                    for dep in it["waits"]:
                        if dep in dassign:
                            s, v = dassign[dep]
                            eng.wait_ge(s, v)
                        else:
                            eng.wait_ge(sems[dep[0]], cnt[dep])
                    if it["fn"] is None:
                        continue
                    ins = it["fn"](eng)
                    if it["sig"]:
                        if it["dma"]:
                            ins.then_inc(dassign[(engname, i)][0], 16)
                        else:
                            ins.then_inc(sems[engname], 1)

            @block.tensor
            def _(eng):
                run("pe", eng)

            @block.scalar
            def _(eng):
                run("act", eng)

            @block.vector
            def _(eng):
                run("dve", eng)

            @block.gpsimd
            def _(eng):
                run("pool", eng)

            @block.sync
            def _(eng):
                run("sp", eng)


class Tl:
    def __init__(self, ap, keys):
        self.ap, self.keys = ap, list(keys)

    def __getitem__(self, idx):
        return Tl(self.ap[idx], self.keys)

    def r(self, pat, **kw):
        return Tl(self.ap.rearrange(pat, **kw), self.keys)

    def bc(self, shape):
        return Tl(self.ap.to_broadcast(shape), self.keys)

    def bitcast(self, dt):
        return Tl(self.ap.bitcast(dt), self.keys)


def _keys(*ts):
    out = []
    for t in ts:
        if t is None or isinstance(t, (int, float)):
            continue
        if isinstance(t, Tl):
            out += t.keys
        elif isinstance(t, str):
            out.append(t)
    return out


def _a(x):
    return x.ap if isinstance(x, Tl) else x


def _col(v):
    return np.ascontiguousarray(np.asarray(v, np.float32).reshape(-1, 128).T)


def _slab_cols(W, cols):
    S = np.zeros((1024, 512), np.float32)
    idx = np.asarray(cols)
    val = idx >= 0
    S[:, np.nonzero(val)[0]] = W[:, idx[val]]
    return S.reshape(8, 128, 512).transpose(1, 0, 2).reshape(128, 4096)


CST = {}


def _cst_layout():
    names = [("gmix0", 8), ("gmlp0", 8), ("gmix1", 8), ("gmlp1", 8), ("gfin", 8), ("mu", 14),
             ("w0", 4), ("a0", 4), ("k_k", 4), ("k_a", 4), ("r_k", 4), ("ln_w", 4), ("ln_b", 4),
             ("cw0", 8), ("cw1", 8), ("cw2", 8), ("cw3", 8), ("cb", 8), ("cnw", 8), ("ib", 1), ("fb", 1)]
    o = 0
    for n, w in names:
        CST[n] = (o, w)
        o += w
    return o


NCST = _cst_layout()
CF_IDF = 0
NCF = 128
CB_ID, CB_ONES, CB_BONES, CB_MLM, CB_MS, CB_MI, CB_MST, CB_RET, CB_SEL, CB_RST = 0, 128, 256, 384, 896, 1024, 1152, 1280, 1408, 2432
NCB = 2432 + 512


def _const_tables(T):
    cf = np.zeros((128, NCF), np.float32)
    cf[:, CF_IDF:CF_IDF + 128] = np.eye(128, dtype=np.float32)
    p = np.arange(128)
    hs, s = p // 64, p % 64
    same = (hs[:, None] == hs[None, :])
    strict = same & (s[:, None] < s[None, :])
    incl = same & (s[:, None] <= s[None, :])
    strictT = same & (s[:, None] > s[None, :])
    ret = (p[:, None] <= p[None, :]).astype(np.float32)
    rst = np.ones(512, np.float32)
    rst[::64] = 0.0
    cb = np.zeros((128, NCB), np.float32)
    cb[:, CB_ID:CB_ID + 128] = np.eye(128)
    cb[:, CB_ONES:CB_ONES + 128] = 1.0
    cb[:, CB_BONES:CB_BONES + 128] = same.astype(np.float32)
    cb[:, CB_MLM:CB_MLM + 512] = np.tile((p[:, None] > p[None, :]).astype(np.float32) * 30000.0, (1, 4))
    cb[:, CB_MS:CB_MS + 128] = strict
    cb[:, CB_MI:CB_MI + 128] = incl
    cb[:, CB_MST:CB_MST + 128] = strictT
    cb[:, CB_RET:CB_RET + 128] = ret
    for j in range(8):
        cb[j, CB_SEL + j * 128: CB_SEL + (j + 1) * 128] = 1.0
    cb[:, CB_RST:CB_RST + 512] = rst[None, :]
    d = 128
    inv = (np.float32(10000.0) ** (-(np.arange(0, d, 2, dtype=np.float32) / np.float32(d)))).astype(np.float32)
    pos = np.arange(T, dtype=np.float32)
    ang = (pos[:, None] * inv[None, :]).astype(np.float32).astype(np.float64)
    cos = np.cos(ang).T
    sin = np.sin(ang).T
    cos2 = np.concatenate([cos, cos], 0)
    sin2 = np.concatenate([-sin, sin], 0)
    l1 = (np.arange(T) % 128 + 1).astype(np.float64)
    rot = np.zeros((16, 128, T), np.float32)
    for h in range(4):
        lg = math.log1p(-2.0 ** (-5.0 - h))
        dq = np.exp(lg * l1)[None, :]
        dk = np.exp(-lg * l1)[None, :] * (128.0 ** -0.5)
        rot[h * 4 + 0] = cos2 * dq
        rot[h * 4 + 1] = sin2 * dq
        rot[h * 4 + 2] = cos2 * dk
        rot[h * 4 + 3] = sin2 * dk
    return cf, cb.astype(ml_dtypes.bfloat16), rot


def _prep_shared(inp, T):
    f = np.float32
    Wab = np.asarray(inp["ab_w_in"][0], f)
    slabs = []
    ar = np.arange
    slabs.append(_slab_cols(Wab, ar(0, 512)))
    slabs.append(_slab_cols(Wab, ar(512, 1024)))
    slabs.append(_slab_cols(Wab, ar(1024, 1536)))
    slabs.append(_slab_cols(Wab, list(ar(1536, 1792)) + [-1] * 256))
    rb = 1792
    dd = ar(128)
    sw = np.where(dd < 64, dd + 64, dd - 64)
    qcols = np.concatenate([rb + h * 128 + dd for h in range(4)])
    qsw = np.concatenate([rb + h * 128 + sw for h in range(4)])
    slabs.append(_slab_cols(Wab, qcols))
    slabs.append(_slab_cols(Wab, qsw))
    slabs.append(_slab_cols(Wab, qcols + 512))
    slabs.append(_slab_cols(Wab, qsw + 512))
    slabs.append(_slab_cols(Wab, ar(rb + 1024, rb + 1536)))
    slabs.append(_slab_cols(Wab, ar(rb + 1536, rb + 2048)))
    Wo = np.asarray(inp["ab_w_out"][0], f)
    slabs += [_slab_cols(Wo, ar(0, 512)), _slab_cols(Wo, ar(512, 1024))]

    def mlp_slabs(l):
        w1 = np.asarray(inp["mlp_w1"][l], f)
        w2 = np.asarray(inp["mlp_w2"][l], f)
        out = [_slab_cols(w1, ar(j * 512, (j + 1) * 512)) for j in range(8)]
        w2r = w2.reshape(32, 128, 8, 128).transpose(1, 0, 2, 3)
        out += [np.ascontiguousarray(w2r[:, :, dj, :]).reshape(128, 4096) for dj in range(8)]
        return out
    slabs += mlp_slabs(0)
    Wc = np.asarray(inp["c_w_in"][0], f)
    slabs += [_slab_cols(Wc, ar(j * 512, (j + 1) * 512)) for j in range(6)]
    Wco = np.asarray(inp["c_w_out"][0], f)
    slabs += [_slab_cols(Wco, ar(0, 512)), _slab_cols(Wco, ar(512, 1024))]
    slabs += mlp_slabs(1)
    wsl = np.stack(slabs).astype(f)
    assert wsl.shape[0] == NSLAB
    cst = np.zeros((128, NCST), f)

    def put(n, v):
        o, w = CST[n]
        cst[:, o:o + w] = _col(v)
    put("gmix0", inp["norm_mix_g"][0]); put("gmix1", inp["norm_mix_g"][1])
    put("gmlp0", inp["norm_mlp_g"][0]); put("gmlp1", inp["norm_mlp_g"][1])
    put("gfin", inp["norm_final_g"])
    put("mu", inp["rwkv_mu"][0]); put("w0", inp["rwkv_w0"][0]); put("a0", inp["rwkv_a0"][0])
    put("k_k", inp["rwkv_k_k"][0]); put("k_a", inp["rwkv_k_a"][0])
    put("r_k", np.asarray(inp["rwkv_r_k"][0]).reshape(-1))
    put("ln_w", inp["rwkv_ln_w"][0]); put("ln_b", inp["rwkv_ln_b"][0])
    cw = np.asarray(inp["c_conv_w"][0], f)
    for j in range(4):
        put("cw%d" % j, cw[j])
    put("cb", inp["c_conv_b"][0]); put("cnw", inp["c_norm_w"][0])
    cst[0:8, CST["ib"][0]] = np.asarray(inp["c_i_bias"][0], f)
    cst[0:8, CST["fb"][0]] = np.asarray(inp["c_f_bias"][0], f)
    lora = np.zeros((128, 3, 512), f)
    lora[0:64, 0] = np.asarray(inp["rwkv_w_up"][0], f)
    lora[64:128, 1] = np.asarray(inp["rwkv_a_up"][0], f)
    lora[:, 2] = np.asarray(inp["rwkv_g_up"][0], f)
    gw = np.ascontiguousarray(Wc[:, 3072:3088].reshape(8, 128, 16).transpose(1, 0, 2))
    cf, cb, rot = _const_tables(T)
    return dict(wsl=wsl, cst=cst, lora=lora.reshape(128, 1536), gw=gw.reshape(128, 128), cf=cf, cb=cb, rot=rot)


def build(nseq, T, dbg=None):
    nc = bass.Bass("TRN2", target_bir_lowering=False)
    P = Prog(nc)
    NG = T // G
    NTOK = nseq * T

    def din(name, shape, dt=F32):
        return nc.dram_tensor(name, shape, dt, kind="ExternalInput").ap()
    x_d = din("x", [NTOK, D])
    wsl_d = din("wsl", [NSLAB, 128, 4096])
    cst_d = din("cst", [128, NCST])
    lora_d = din("lora", [128, 1536])
    gw_d = din("gw", [128, 128])
    cf_d = din("cf", [128, NCF])
    cb_d = din("cb", [128, NCB], BF16)
    rot_d = din("rot", [16, 128, T])
    out_d = nc.dram_tensor("out", [NTOK, D], F32, kind="ExternalOutput").ap()
    wbf_d = nc.dram_tensor("wbf", [NSLAB, 128, 4096], BF16, kind="Internal").ap()
    dbg_d = None
    if dbg is not None:
        dbg_d = nc.dram_tensor("dbg", [128, dbg], F32, kind="ExternalOutput").ap()

    def sb(name, shape, dt=F32):
        return Tl(nc.alloc_sbuf_tensor("sb_" + name, shape, dt).ap(), [name])

    NF, NB = 38, 25
    FP = nc.alloc_sbuf_tensor("FP", [128, NF * 512], F32).ap()
    BP = nc.alloc_sbuf_tensor("BP", [128, NB * 1024], BF16).ap()
    PSA = nc.alloc_psum_tensor("PS", [128, 4096], F32).ap()

    def Ft(i, n=1):
        return Tl(FP[:, i * 512:(i + n) * 512], ["F%d" % j for j in range(i, i + n)])

    def Bt(i, n=1):
        return Tl(BP[:, i * 1024:(i + n) * 1024], ["B%d" % j for j in range(i, i + n)])

    def Bh(i, half):
        return Tl(BP[:, i * 1024 + half * 512: i * 1024 + half * 512 + 512], ["B%d" % i])

    def PSb(b, n=1):
        return Tl(PSA[:, b * 512:(b + n) * 512], ["ps%d" % j for j in range(b, b + n)])

    psctr = [0]

    def pb(n=1):
        b = psctr[0]
        if b + n > 8:
            b = 0
        psctr[0] = (b + n) % 8
        return PSb(b, n)

    def sbk(name, shape, dt=F32):
        return Tl(nc.alloc_sbuf_tensor("sb_" + name, shape, dt).ap(), ["%s%d" % (name, i) for i in range(shape[1])])

    def ck(t_, i):
        return Tl(t_.ap[:, i, :], [t_.keys[i]])
    rwS = sbk("rwS", [128, 4, 128], BF16)
    xT = sbk("xT", [128, 8, 512])
    hT = sbk("hT", [128, 8, 512], BF16)
    yT = sbk("yT", [128, 8, 512], BF16)
    ring = [sb("ring%d" % i, [128, 4096], BF16) for i in range(NSLOT)]
    cst = sb("cst", [128, NCST])
    cst2 = sb("cst2", [128, 24])
    cf = sb("cf", [128, NCF])
    cb = sb("cb", [128, NCB], BF16)
    lora = sb("lora", [128, 1536], BF16)
    gw = sb("gw", [128, 128], BF16)
    rwcar = sb("rwcar", [128, 14])
    rtS32 = sb("rtS32", [128, 4, 128])
    rtSb = sb("rtSb", [128, 4, 128], BF16)
    mlS32 = sb("mlS32", [128, 4, 256])
    mlSb = sb("mlSb", [128, 4, 256], BF16)
    mlcar = sb("mlcar", [128, 8, 3])
    mlrow = sb("mlrow", [8, 2])
    kwpad = [sb("kwpad%d" % i, [128, 4, 128], BF16) for i in range(2)]
    small = sb("small", [128, 64])
    wl_t = [sb("wl%d" % i, [128, 8]) for i in range(2)]
    sb_gc = sb("gc", [128, 64])

    def C(name, i=0, n=1):
        o, w = CST[name]
        return cst[:, o + i:o + i + n]

    def mm(o, l, r, st=True, sp=True):
        P.op("pe", lambda e: e.matmul(o.ap, lhsT=l.ap, rhs=r.ap, start=st, stop=sp), reads=_keys(l, r), writes=_keys(o))

    def tp(o, i, ident):
        P.op("pe", lambda e: e.transpose(o.ap, i.ap, ident.ap), reads=_keys(i, ident), writes=_keys(o))

    def act(o, i, f, scale=None, bias=None):
        kw = {}
        if scale is not None:
            kw["scale"] = _a(scale)
        if bias is not None:
            kw["bias"] = _a(bias)
        P.op("act", lambda e: e.activation(out=o.ap, in_=i.ap, func=f, **kw), reads=_keys(i, scale, bias), writes=_keys(o))

    def tt(o, a, b, op, eng="dve"):
        P.op(eng, lambda e: e.tensor_tensor(out=o.ap, in0=a.ap, in1=b.ap, op=op), reads=_keys(a, b), writes=_keys(o))

    def ts(o, a, s1, op0, s2=None, op1=None, eng="dve"):
        if op1 is None:
            P.op(eng, lambda e: e.tensor_scalar(out=o.ap, in0=a.ap, scalar1=_a(s1), scalar2=None, op0=op0),
                 reads=_keys(a, s1), writes=_keys(o))
        else:
            P.op(eng, lambda e: e.tensor_scalar(out=o.ap, in0=a.ap, scalar1=_a(s1), scalar2=_a(s2), op0=op0, op1=op1),
                 reads=_keys(a, s1, s2), writes=_keys(o))

    def stt(o, a, s, b, op0, op1):
        P.op("dve", lambda e: e.scalar_tensor_tensor(out=o.ap, in0=a.ap, scalar=_a(s), in1=b.ap, op0=op0, op1=op1),
             reads=_keys(a, s, b), writes=_keys(o))

    def scan(o, d0, d1, init, op0, op1):
        P.op("dve", lambda e: e.tensor_tensor_scan(out=o.ap, data0=d0.ap, data1=d1.ap, initial=_a(init), op0=op0, op1=op1),
             reads=_keys(d0, d1, init), writes=_keys(o))

    def red(o, i, op):
        P.op("dve", lambda e: e.tensor_reduce(out=o.ap, in_=i.ap, axis=AX.X, op=op), reads=_keys(i), writes=_keys(o))

    def cp(o, i, eng="dve"):
        if eng == "act":
            P.op("act", lambda e: e.copy(out=o.ap, in_=i.ap), reads=_keys(i), writes=_keys(o))
        else:
            P.op(eng, lambda e: e.tensor_copy(out=o.ap, in_=i.ap), reads=_keys(i), writes=_keys(o))

    def mset(o, v, eng="pool"):
        P.op(eng, lambda e: e.memset(o.ap, v), writes=_keys(o))

    def dma(o, i, sem, eng="sp", rk=(), wk=()):
        return P.op(eng, lambda e: e.dma_start(out=_a(o), in_=_a(i)), reads=_keys(i) + list(rk), writes=_keys(o) + list(wk), dma=sem)

    idf = cf[:, CF_IDF:CF_IDF + 128]
    idb = cb[:, CB_ID:CB_ID + 128]
    ones = cb[:, CB_ONES:CB_ONES + 128]
    bones = cb[:, CB_BONES:CB_BONES + 128]
    mlmask = cb[:, CB_MLM:CB_MLM + 512]

    def mask4(off):
        m = cb[:, off:off + 128]
        return Tl(m.ap.unsqueeze(1).to_broadcast([128, 4, 128]), m.keys)

    def run_jobs(jobs, width=2):
        jobs = list(jobs)
        active = []
        nxt = 0
        free = list(range(width))
        while active or nxt < len(jobs):
            while free and nxt < len(jobs):
                sl_ = free.pop(0)
                active.append((sl_, jobs[nxt](sl_)))
                nxt += 1
            for ent in list(active):
                try:
                    next(ent[1])
                except StopIteration:
                    active.remove(ent)
                    free.append(ent[0])

    def slot_ps(slot):
        ctr = [0]

        def rot(n=1):
            b = ctr[0] % 4
            if b + n > 4:
                b = 0
            ctr[0] = b + n
            return PSb(4 * slot + b, n)
        return rot, (lambda i, n=1: PSb(4 * slot + i, n))

    dma(cst, cst_d, "cst"); dma(cf, cf_d, "cf"); dma(cb, cb_d, "cb")
    l32 = Ft(0, 3)
    dma(l32, lora_d, "l32")
    cp(lora, l32)
    g32 = Ft(3)[:, 0:128]
    dma(g32, gw_d, "g32")
    cp(gw, g32)
    o, w = CST["mu"]
    ts(cst2[:, 0:14], cst[:, o:o + 14], -1.0, ALU.mult, 1.0, ALU.add)
    o, w = CST["k_a"]
    ts(cst2[:, 14:18], cst[:, o:o + 4], -1.0, ALU.mult, 1.0, ALU.add)
    ts(cst2[:, 18:19], C("ib"), 1.0 / 15.0, ALU.mult)
    ts(cst2[:, 19:20], C("fb"), 1.0 / 15.0, ALU.mult)
    for t_ in kwpad:
        mset(t_, 0.0)
    xtm0 = Ft(30, 8).r("p (j d) -> p j d", j=4)
    dma(xtm0, x_d[0:G, :].rearrange("(j p) d -> p j d", p=128), "xtm", eng="pool")
    for s_ in range(NSLAB):
        dma(wbf_d[s_], wsl_d[s_], "wc%d" % s_, eng="pool", wk=["wbf%d" % s_])

    ringctr = [0]

    def wload(slab):
        t_ = ring[ringctr[0] % NSLOT]
        ringctr[0] += 1
        dma(t_, wbf_d[slab], t_.keys[0], rk=["wbf%d" % slab])
        return t_

    def rmsnorm(gname, out_bf=True, outs=None):
        sq = Bt(0, 4)
        act(sq, xT.r("p a b -> p (a b)"), AF.Square)
        ps = pb()
        for kc in range(8):
            mm(ps, ones, sq[:, kc * 512:(kc + 1) * 512], st=(kc == 0), sp=(kc == 7))
        rstd = Ft(0)
        act(rstd, ps, AF.Ln, scale=1.0 / D, bias=1e-6)
        act(rstd, rstd, AF.Exp, scale=-0.5)
        for kc in range(8):
            dst = ck(hT, kc) if out_bf else outs[kc]
            stt(dst, ck(xT, kc), C(gname, kc), rstd, ALU.mult, ALU.mult)

    def proj_fm(slab, chunks, evac):
        wt = wload(slab).r("p (k c) -> p k c", k=8)
        for jc in chunks:
            ps = pb()
            for kc in range(8):
                mm(ps, wt[:, kc, jc * 128:(jc + 1) * 128], ck(hT, kc), st=(kc == 0), sp=(kc == 7))
            evac(jc, ps)

    def proj_tm(slab, evac):
        wt = wload(slab).r("p (k c) -> p k c", k=8)
        for j in range(4):
            ps = pb()
            for kc in range(8):
                mm(ps, ck(hT, kc)[:, j * 128:(j + 1) * 128], wt[:, kc, :], st=(kc == 0), sp=(kc == 7))
            evac(j, ps)

    def outproj(slabs, src):
        for si, slab in enumerate(slabs):
            wt = wload(slab).r("p (k c) -> p k c", k=8)
            for jc in range(4):
                ps = pb()
                for kc in range(8):
                    mm(ps, wt[:, kc, jc * 128:(jc + 1) * 128], ck(src, kc), st=(kc == 0), sp=(kc == 7))
                dc = si * 4 + jc
                tt(ck(xT, dc), ck(xT, dc), ps, ALU.add)

    def mlp(layer, w1s, w2s):
        rmsnorm("gmlp%d" % layer)
        hid = Bt(4, 16).r("p (f t) -> p f t", f=32)
        for j in range(8):
            wt = wload(w1s + j).r("p (k c) -> p k c", k=8)
            for jc in range(4):
                ps = pb()
                for kc in range(8):
                    mm(ps, wt[:, kc, jc * 128:(jc + 1) * 128], ck(hT, kc), st=(kc == 0), sp=(kc == 7))
                f_ = j * 4 + jc
                tmp = Ft(1 + (f_ % 2))
                act(tmp, ps, AF.Relu)
                tt(hid[:, f_, :], tmp, tmp, ALU.mult, eng="pool")
        for dj in range(8):
            wt = wload(w2s + dj).r("p (f c) -> p f c", f=32)
            ps = pb()
            for f_ in range(32):
                mm(ps, wt[:, f_, :], hid[:, f_, :], st=(f_ == 0), sp=(f_ == 31))
            tt(ck(xT, dj), ck(xT, dj), ps, ALU.add)

    def rwkv_stage(first):
        if first:
            mset(rwcar, 0.0)
            mset(rwS, 0.0)
        rT = [Ft(1 + i) for i in range(4)]
        kT = [Ft(5 + i) for i in range(4)]
        vT = [Ft(9 + i) for i in range(4)]
        waT, gloT = Ft(13), Ft(14)

        def lerp(col, ps, dst):
            tmpl = Ft(19) if col % 2 == 0 else Ft(30)
            act(tmpl, ps, AF.Identity, scale=cst2[:, col:col + 1])
            stt(dst[:, 1:512], ps[:, 0:511], C("mu", col), tmpl[:, 1:512], ALU.mult, ALU.add)
            stt(dst[:, 0:1], rwcar[:, col:col + 1], C("mu", col), tmpl[:, 0:1], ALU.mult, ALU.add)
            cp(rwcar[:, col:col + 1], ps[:, 511:512], "act")
        proj_fm(0, range(4), lambda jc, ps: lerp(jc, ps, rT[jc]))
        proj_fm(1, range(4), lambda jc, ps: lerp(4 + jc, ps, kT[jc]))
        proj_fm(2, range(4), lambda jc, ps: lerp(8 + jc, ps, vT[jc]))
        proj_fm(3, range(2), lambda jc, ps: lerp(12 + jc, ps, waT if jc == 0 else gloT))
        lo = Bt(0)
        act(lo[0:64, 0:512], waT[0:64, :], AF.Tanh)
        act(lo[64:128, 0:512], waT[64:128, :], AF.Identity)
        act(lo[:, 512:1024], gloT, AF.Sigmoid)
        lora3 = lora.r("p (a c) -> p a c", a=3)
        BDN = ("At", "Rt", "Bt", "Kt", "Vt", "BWt", "KWt")

        def pair_job(p_):
            def gen(slot):
                rot, fix = slot_ps(slot)
                fb = 13 + 12 * slot
                yfm, gTp, t1, t2, t3, t4, kkn, bb, bon, Yl, Ys, sq = [Ft(fb + i) for i in range(12)]
                lw, av, cr = Yl, Ys, sq
                b0 = 1 + 12 * slot
                kk2, rk = Bh(b0, 0), Bh(b0, 1)
                bd = {n: Bh(b0 + 1 + i // 2, i % 2).r("p (c t) -> p c t", c=4) for i, n in enumerate(BDN)}
                Ark = Bh(b0 + 4, 1)
                for n in BDN:
                    mset(bd[n], 0.0, eng="pool")
                cs_ = slice(p_ * 128, (p_ + 1) * 128)
                r_, k_, v_ = rT[p_], kT[p_], vT[p_]
                ps = rot()
                mm(ps, lora3[0:64, 0, cs_], lo[0:64, 0:512])
                act(lw, ps, AF.Sigmoid, bias=C("w0", p_))
                yield
                ps = rot()
                mm(ps, lora3[64:128, 1, cs_], lo[64:128, 0:512])
                act(av, ps, AF.Sigmoid, bias=C("a0", p_))
                yield
                ps = rot()
                mm(ps, lora3[:, 2, cs_], lo[:, 512:1024])
                cp(gTp, ps, "act")
                act(kk2, k_, AF.Square, scale=C("k_k", p_))
                yield
                ps = rot()
                mm(ps, bones, kk2)
                rs = bon
                ts(rs, ps, 1e-24, ALU.max)
                yield
                act(rs, rs, AF.Ln)
                scan(cr, cb[:, CB_RST:CB_RST + 512], lw, 0.0, ALU.mult, ALU.add)
                yield
                act(rs, rs, AF.Exp, scale=-0.5)
                tt(t1, cr, lw, ALU.subtract)
                yield
                stt(kkn, k_, C("k_k", p_), rs, ALU.mult, ALU.mult)
                act(t1, t1, AF.Exp, scale=-KAPPA)
                yield
                t2p = yfm
                ts(t2p, av, C("k_a", p_), ALU.mult, cst2[:, 14 + p_:15 + p_], ALU.add)
                act(t2, cr, AF.Exp, scale=KAPPA)
                yield
                tt(k_, k_, t2p, ALU.mult)
                act(t3, cr, AF.Exp, scale=-KAPPA)
                yield
                tt(bb, kkn, av, ALU.mult)
                cr3 = cr.r("p (c t) -> p c t", t=64)
                tt(t4.r("p (c t) -> p c t", t=64), cr3, cr3[:, :, 63:64].bc([128, 8, 64]), ALU.subtract)
                yield
                act(t4, t4, AF.Exp, scale=KAPPA)
                WL = small[:, 56 + 4 * 0: 56 + 4 * 0] if False else wl_t[slot]
                act(WL, cr3[:, :, 63], AF.Exp, scale=-KAPPA)
                stt(rk, r_, C("r_k", p_), k_, ALU.mult, ALU.mult)
                yield
                ps = rot()
                mm(ps, bones, rk)
                tt(bon, ps, v_, ALU.mult)
                yield
                for hf in range(2):
                    c0 = hf * 4
                    for h_ in range(2):
                        hp = slice(h_ * 64, (h_ + 1) * 64)
                        fs = slice(h_ * 64, (h_ + 1) * 64)

                        def v3(t_):
                            return t_[hp, c0 * 64:(c0 + 4) * 64].r("p (c t) -> p c t", t=64)
                        stt(bd["At"][hp, :, fs], v3(kkn), -1.0, v3(t1), ALU.mult, ALU.mult)
                        tt(bd["Rt"][hp, :, fs], v3(r_), v3(t3), ALU.mult)
                        tt(bd["Bt"][hp, :, fs], v3(bb), v3(t2), ALU.mult)
                        yield
                        tt(bd["Kt"][hp, :, fs], v3(k_), v3(t2), ALU.mult)
                        cp(bd["Vt"][hp, :, fs], v3(v_), "act")
                        tt(bd["BWt"][hp, :, fs], v3(bb), v3(t4), ALU.mult, eng="pool")
                        tt(bd["KWt"][hp, :, fs], v3(k_), v3(t4), ALU.mult)
                        yield
                    Vtm, BWtm, KWtm = Bh(b0 + 5, 0), Bh(b0 + 5, 1), Bh(b0 + 6, 0)
                    X = Bt(b0 + 7).r("p (c x) -> p c x", c=4)
                    for kind, dst in (("Vt", Vtm), ("BWt", BWtm), ("KWt", KWtm), ("At", None)):
                        psb_ = rot().bitcast(BF16)[:, 0:512]
                        for c in range(4):
                            tp(psb_[:, c * 128:(c + 1) * 128], bd[kind][:, c, :], idb)
                        if dst is not None:
                            cp(dst, psb_, "act")
                        else:
                            cp(X[:, :, 0:128], psb_.r("p (c t) -> p c t", c=4), "act")
                        yield
                    Mj, Nj = Bh(b0 + 8, 0), Bh(b0 + 8, 1)
                    Aak, Arb = Bh(b0 + 10, 0), Bh(b0 + 10, 1)
                    specs = (("Bt", "At", Mj, CB_MS), ("At", "Bt", Nj, CB_MST), ("Kt", "At", Aak, CB_MS),
                             ("Bt", "Rt", Arb, CB_MI), ("Kt", "Rt", Ark, CB_MI))
                    for (l_, r2_, dst, mo) in specs:
                        ps = rot()
                        for c in range(4):
                            mm(ps[:, c * 128:(c + 1) * 128], bd[l_][:, c, :], bd[r2_][:, c, :])
                        tt(dst.r("p (c t) -> p c t", c=4), ps.r("p (c t) -> p c t", c=4), mask4(mo), ALU.mult)
                        yield
                    ps = rot()
                    for c in range(4):
                        mm(ps[:, c * 128:(c + 1) * 128], Aak[:, c * 128:(c + 1) * 128], Vtm[:, c * 128:(c + 1) * 128])
                    cp(X[:, :, 128:256], ps.r("p (c t) -> p c t", c=4), "act")
                    yield
                    Xf = X.r("p c x -> p (c x)")
                    for j in range(6):
                        ps2 = fix(0, 2)
                        for c in range(4):
                            mm(ps2[:, c * 256:(c + 1) * 256], idb, X[:, c, :], st=True, sp=False)
                            mm(ps2[:, c * 256:(c + 1) * 256], Mj[:, c * 128:(c + 1) * 128], X[:, c, :], st=False, sp=True)
                        if j < 5:
                            psm, psn = fix(2), fix(3)
                            for c in range(4):
                                sl = slice(c * 128, (c + 1) * 128)
                                mm(psm[:, sl], Nj[:, sl], Mj[:, sl])
                                mm(psn[:, sl], Mj[:, sl], Nj[:, sl])
                        cp(Xf, ps2, "act")
                        if j < 5:
                            Mn, Nn = (Bh(b0 + 9, 0), Bh(b0 + 9, 1)) if j % 2 == 0 else (Bh(b0 + 8, 0), Bh(b0 + 8, 1))
                            cp(Mn, psm, "act")
                            cp(Nn, psn, "dve")
                            Mj, Nj = Mn, Nn
                        yield
                    GT, QT = Bh(b0 + 11, 0), Bh(b0 + 11, 1)
                    Hb = Bh(b0 + 8, 0)
                    ps = rot()
                    for c in range(4):
                        mm(ps[:, c * 128:(c + 1) * 128], X[:, c, 0:128], BWtm[:, c * 128:(c + 1) * 128])
                    for c in range(4):
                        sl = slice(c * 128, (c + 1) * 128)
                        stt(GT[:, sl], idf, WL[:, c0 + c:c0 + c + 1], ps[:, sl], ALU.mult, ALU.add)
                    yield
                    ps = rot()
                    for c in range(4):
                        sl = slice(c * 128, (c + 1) * 128)
                        mm(ps[:, sl], BWtm[:, sl], X[:, c, 128:256], st=True, sp=False)
                        mm(ps[:, sl], KWtm[:, sl], Vtm[:, sl], st=False, sp=True)
                    cp(Hb, ps, "act")
                    yield
                    ps = rot()
                    for c in range(4):
                        sl = slice(c * 128, (c + 1) * 128)
                        mm(ps[:, sl], idb, bd["Rt"][:, c, :], st=True, sp=False)
                        mm(ps[:, sl], X[:, c, 0:128], Arb[:, sl], st=False, sp=True)
                    cp(QT, ps, "act")
                    yield
                    ps = rot()
                    for c in range(4):
                        sl = slice(c * 128, (c + 1) * 128)
                        mm(ps[:, sl], Arb[:, sl], X[:, c, 128:256], st=True, sp=False)
                        mm(ps[:, sl], Ark[:, sl], Vtm[:, sl], st=False, sp=True)
                    cp(Yl, ps, "act")
                    yield
                    psy = fix(0)
                    for c in range(4):
                        sl = slice(c * 128, (c + 1) * 128)
                        mm(psy[:, sl], QT[:, sl], ck(rwS, p_))
                        pss = fix(1 + c % 2)
                        mm(pss[:, 0:128], GT[:, sl], ck(rwS, p_), st=True, sp=False)
                        mm(pss[:, 0:128], idb, Hb[:, sl], st=False, sp=True)
                        cp(ck(rwS, p_), pss[:, 0:128], "act")
                        yield
                    tt(Ys, psy, Yl, ALU.add)
                    st_ = small[:, 8 + 24 * slot:32 + 24 * slot]
                    Ys3 = Ys.r("p (c v) -> p c v", c=4)
                    red(st_[:, 0:4], Ys3, ALU.add)
                    act(sq, Ys, AF.Square)
                    red(st_[:, 4:8], sq.r("p (c v) -> p c v", c=4), ALU.add)
                    yield
                    ts(st_[:, 8:12], st_[:, 0:4], 1.0 / 64, ALU.mult)
                    tt(st_[:, 12:16], st_[:, 8:12], st_[:, 8:12], ALU.mult)
                    stt(st_[:, 16:20], st_[:, 4:8], 1.0 / 64, st_[:, 12:16], ALU.mult, ALU.subtract)
                    act(st_[:, 16:20], st_[:, 16:20], AF.Ln, bias=64e-5)
                    act(st_[:, 20:24], st_[:, 16:20], AF.Exp, scale=-0.5)
                    yield
                    stt(st_[:, 12:16], st_[:, 8:12], -1.0, st_[:, 20:24], ALU.mult, ALU.mult)
                    for c in range(4):
                        sl = slice(c * 128, (c + 1) * 128)
                        act(sq[:, sl], Ys[:, sl], AF.Identity, scale=st_[:, 20 + c:21 + c], bias=st_[:, 12 + c:13 + c])
                    ps = fix(3)
                    for c in range(4):
                        sl = slice(c * 128, (c + 1) * 128)
                        tp(ps[:, sl], sq[:, sl], idf)
                    ps3 = ps.r("p (c t) -> p c t", c=4)
                    for h_ in range(2):
                        hp = slice(h_ * 64, (h_ + 1) * 64)
                        act(yfm[hp, hf * 256:(hf + 1) * 256].r("p (c t) -> p c t", c=4), ps3[hp, :, h_ * 64:(h_ + 1) * 64],
                            AF.Identity, scale=C("ln_w", p_)[hp, :], bias=C("ln_b", p_)[hp, :])
                    yield
                tt(yfm, yfm, bon, ALU.add)
                tt(ck(yT, p_), yfm, gTp, ALU.mult)
                yield
            return gen
        run_jobs([pair_job(p_) for p_ in range(4)])

    def ret_stage(first, t0):
        if first:
            mset(rtS32, 0.0)
            mset(rtSb, 0.0)
        qh = Bt(0, 2).r("p (h t) -> p h t", h=4)
        kh = Bt(2, 2).r("p (h t) -> p h t", h=4)
        vtm = Bt(4, 2).r("p (j c) -> p j c", j=4)
        ktm = Bt(6, 2).r("p (j c) -> p j c", j=4)
        sg = Bt(8, 2).r("p (h t) -> p h t", h=4)
        psq = {}

        def rot(h_, which, psa, psb_, dst):
            tabs = []
            for kd in range(2):
                tb = Ft((1 if h_ % 2 == 0 else 15) + kd + 2 * which)
                dma(tb, rot_d[h_ * 4 + which * 2 + kd][:, t0:t0 + G], tb.keys[0], eng="pool")
                tabs.append(tb)
            u1, u2 = (Ft(5), Ft(6)) if h_ % 2 == 0 else (Ft(13), Ft(14))
            tt(u1, psa, tabs[0], ALU.mult)
            tt(u2, psb_, tabs[1], ALU.mult)
            tt(dst, u1, u2, ALU.add, eng="pool")
        for which, (sa_, sb_, dst) in enumerate(((4, 5, qh), (6, 7, kh))):
            wa_ = wload(sa_).r("p (k c) -> p k c", k=8)
            wb_ = wload(sb_).r("p (k c) -> p k c", k=8)
            for h_ in range(4):
                pa, pb_ = pb(), pb()
                for kc in range(8):
                    mm(pa, wa_[:, kc, h_ * 128:(h_ + 1) * 128], ck(hT, kc), st=(kc == 0), sp=(kc == 7))
                for kc in range(8):
                    mm(pb_, wb_[:, kc, h_ * 128:(h_ + 1) * 128], ck(hT, kc), st=(kc == 0), sp=(kc == 7))
                rot(h_, which, pa, pb_, dst[:, h_, :])
        proj_tm(8, lambda j, ps: cp(vtm[:, j, :], ps, "act"))
        proj_fm(9, range(4), lambda jc, ps: act(sg[:, jc, :], ps, AF.Silu))
        for h_ in range(4):
            psb_ = pb().bitcast(BF16)[:, 0:512]
            for j in range(4):
                tp(psb_[:, j * 128:(j + 1) * 128], kh[:, h_, j * 128:(j + 1) * 128], idb)
            cp(ktm[:, :, h_ * 128:(h_ + 1) * 128], psb_.r("p (j d) -> p j d", j=4), "act")
        def ret_head(h_):
            def gen(slot):
                rot, fix = slot_ps(slot)
                gl = math.exp(128.0 * math.log1p(-2.0 ** (-5.0 - h_)))
                hs_ = slice(h_ * 128, (h_ + 1) * 128)
                ps = fix(3)
                for j in range(4):
                    sl = slice(j * 128, (j + 1) * 128)
                    mm(ps[:, sl], kh[:, h_, sl], qh[:, h_, sl])
                ST, osq = Bh(10 + slot, 0), Bh(10 + slot, 1)
                tt(ST.r("p (c t) -> p c t", c=4), ps.r("p (c t) -> p c t", c=4), mask4(CB_RET), ALU.mult)
                yield
                pso = fix(0)
                psr = fix(1)
                for j in range(4):
                    sl = slice(j * 128, (j + 1) * 128)
                    mm(psr[:, sl], ktm[:, j, hs_], vtm[:, j, hs_])
                yield
                for j in range(4):
                    sl = slice(j * 128, (j + 1) * 128)
                    mm(pso[:, sl], vtm[:, j, hs_], ST[:, sl], st=True, sp=False)
                    mm(pso[:, sl], rtSb[:, h_, :], qh[:, h_, sl], st=False, sp=True)
                    tmp = Ft(7 + 3 * slot)[:, 0:128]
                    tt(tmp, psr[:, sl], rtS32[:, h_, :], ALU.add)
                    act(rtS32[:, h_, :], tmp, AF.Identity, scale=gl)
                    act(rtSb[:, h_, :], tmp, AF.Identity, scale=gl)
                    yield
                o32 = Ft(8 + 3 * slot)
                cp(o32, pso, "act")
                act(osq, pso, AF.Square)
                psm = fix(3)
                mm(psm, ones, osq)
                rs = Ft(9 + 3 * slot)
                act(rs, psm, AF.Ln, scale=1.0 / 128, bias=1e-6)
                act(rs, rs, AF.Exp, scale=-0.5)
                yield
                tt(o32, o32, rs, ALU.mult)
                tt(ck(yT, 4 + h_), o32, sg[:, h_, :], ALU.mult)
                yield
            return gen
        run_jobs([ret_head(h_) for h_ in range(4)])

    def mlstm_stage(first):
        if first:
            for t_ in (mlS32, mlSb, mlcar, mlrow):
                mset(t_, 0.0)
        qT = Bt(0, 2).r("p (c t) -> p c t", c=4)
        kT = Bt(2, 2).r("p (c t) -> p c t", c=4)
        vtm = Bt(4, 4).r("p (j c) -> p j c", j=4)
        so = Bt(8, 4).r("p (h t) -> p h t", h=8)
        def conv(ch, ps, dst):
            xe = (Ft(1, 2) if ch % 2 == 0 else Ft(22, 2))[:, 0:515]
            cp(xe[:, 3:515], ps, "act")
            cp(xe[:, 0:3], mlcar[:, ch, :], "act")
            acc = Ft(3) if ch % 2 == 0 else Ft(24)
            ts(acc, xe[:, 0:512], C("cw0", ch), ALU.mult, C("cb", ch), ALU.add)
            for j in range(1, 4):
                stt(acc, xe[:, j:j + 512], C("cw%d" % j, ch), acc, ALU.mult, ALU.add)
            cp(mlcar[:, ch, :], xe[:, 512:515], "act")
            act(dst, acc, AF.Silu)
        proj_fm(28, range(4), lambda jc, ps: conv(jc, ps, qT[:, jc, :]))
        proj_fm(29, range(4), lambda jc, ps: conv(4 + jc, ps, kT[:, jc, :]))
        proj_tm(30, lambda j, ps: cp(vtm[:, j, 0:512], ps, "act"))
        proj_tm(31, lambda j, ps: cp(vtm[:, j, 512:1024], ps, "act"))
        proj_fm(32, range(4), lambda jc, ps: act(so[:, jc, :], ps, AF.Sigmoid))
        proj_fm(33, range(4), lambda jc, ps: act(so[:, 4 + jc, :], ps, AF.Sigmoid))
        gw3 = gw.r("p (k c) -> p k c", k=8)
        psi, psf = pb(), pb()
        for kc in range(8):
            mm(psi[0:8, :], gw3[:, kc, 0:8], ck(hT, kc), st=(kc == 0), sp=(kc == 7))
        for kc in range(8):
            mm(psf[0:8, :], gw3[:, kc, 8:16], ck(hT, kc), st=(kc == 0), sp=(kc == 7))
        R = [Ft(4 + i)[0:8, :] for i in range(8)]
        li, lf, Bc, Gs, Mx, em, inter, wr = R
        act(li, psi[0:8, :], AF.Tanh, scale=1.0 / 15, bias=cst2[0:8, 18:19])
        ts(li, li, 15.0, ALU.mult)
        act(lf, psf[0:8, :], AF.Tanh, scale=1.0 / 15, bias=cst2[0:8, 19:20])
        act(lf, lf, AF.Exp, scale=-15.0)
        act(lf, lf, AF.Ln, bias=1.0)
        negsp = wr
        ts(negsp, lf, -1.0, ALU.mult)
        onesr = em
        mset(onesr, 1.0, eng="dve")
        scan(Bc, onesr, negsp, mlrow[0:8, 0:1], ALU.mult, ALU.add)
        tt(Gs, li, Bc, ALU.subtract)
        Mext = Tl(FP[0:8, 12 * 512:12 * 512 + 513], ["F12", "F13"])
        cp(Mext[:, 0:1], mlrow[0:8, 1:2], "dve")
        scan(Mext[:, 1:513], Gs, Gs, mlrow[0:8, 1:2], ALU.max, ALU.max)
        Mrow = Mext[:, 1:513]
        cp(mlrow[0:8, 0:1], Bc[:, 511:512], "dve")
        cp(mlrow[0:8, 1:2], Mext[:, 512:513], "dve")
        tt(em, Bc, Mrow, ALU.add)
        act(em, em, AF.Exp, scale=-1.0)
        M3 = Mrow.r("p (j t) -> p j t", j=4)
        Mp3 = Mext[:, 0:512].r("p (j t) -> p j t", j=4)[:, :, 0:1]
        Me3 = Mext[:, 1:513].r("p (j t) -> p j t", j=4)[:, :, 127:128]
        tt(inter.r("p (j t) -> p j t", j=4), Mp3.bc([8, 4, 128]), M3, ALU.subtract)
        act(inter, inter, AF.Exp)
        tt(wr.r("p (j t) -> p j t", j=4), Gs.r("p (j t) -> p j t", j=4), Me3.bc([8, 4, 128]), ALU.subtract)
        act(wr, wr, AF.Exp, bias=math.log(0.125))
        hl = {}
        for i_, (nm_, src_) in enumerate((("M", Mrow), ("em", em), ("in", inter))):
            hi_, lo_ = Bt(16 + i_)[0:8, 0:512], Bt(16 + i_)[0:8, 512:1024]
            cp(hi_, src_, "act")
            tt(lo_, src_, hi_, ALU.subtract)
            hl[nm_] = (hi_, lo_)
        csr = small[0:8, 32:36]
        tt(csr.r("p (j o) -> p j o", o=1), Mp3, Me3, ALU.subtract)
        act(csr, csr, AF.Exp)
        cs_hi, cs_lo = Bt(19)[0:8, 0:4], Bt(19)[0:8, 4:8]
        cp(cs_hi, csr, "act")
        tt(cs_lo, csr, cs_hi, ALU.subtract)
        psg = pb()
        for j in range(4):
            tp(psg[:, j * 8:(j + 1) * 8], Gs[:, j * 128:(j + 1) * 128], idf[0:8, 0:8])
            tp(psg[:, 32 + j * 8:32 + (j + 1) * 8], wr[:, j * 128:(j + 1) * 128], idf[0:8, 0:8])
        Gc = sb_gc
        ts(Gc[:, 0:32], psg[:, 0:32], math.log(0.125), ALU.add)
        cp(Gc[:, 32:64], psg[:, 32:64], "dve")
        def ml_head(h_):
            def gen(slot):
                rot, fix = slot_ps(slot)

                def pt():
                    return fix(2 + (ptc[0] % 2))
                ptc = [0]

                def nxt():
                    ptc[0] += 1
                    return fix(2 + (ptc[0] % 2))
                hp = slice((h_ % 2) * 64, (h_ % 2) * 64 + 64)
                pr = h_ // 2
                selb = cb[0:8, CB_SEL + h_ * 128: CB_SEL + (h_ + 1) * 128]
                f0 = 14 + 4 * slot
                b0 = 12 + 2 * slot
                psM = nxt()
                mm(psM, selb, hl["M"][0], st=True, sp=False)
                mm(psM, selb, hl["M"][1], st=False, sp=False)
                mm(psM, idb, mlmask, st=False, sp=True)
                E = Ft(f0)
                for j in range(4):
                    sl = slice(j * 128, (j + 1) * 128)
                    act(E[:, sl], psM[:, sl], AF.Exp, scale=-1.0, bias=Gc[:, j * 8 + h_: j * 8 + h_ + 1])
                yield
                psE = nxt()
                mm(psE, selb, hl["em"][0], st=True, sp=False)
                mm(psE, selb, hl["em"][1], st=False, sp=True)
                emt2 = Ft(f0 + 1)
                act(emt2, psE, AF.Square, scale=1e-3)
                yield
                psI = nxt()
                mm(psI, selb, hl["in"][0], st=True, sp=False)
                mm(psI, selb, hl["in"][1], st=False, sp=True)
                Qi = Bh(b0, 0)
                tt(Qi[hp, :], qT[hp, pr, :], psI[hp, :], ALU.mult)
                yield
                pcs = nxt()
                mm(pcs[:, 0:4], selb, cs_hi, st=True, sp=False)
                mm(pcs[:, 0:4], selb, cs_lo, st=False, sp=True)
                csc = small[:, 4 * slot:4 * slot + 4]
                cp(csc, pcs[:, 0:4], "dve")
                psS = nxt()
                for j in range(4):
                    sl = slice(j * 128, (j + 1) * 128)
                    mm(psS[:, sl], kT[hp, pr, sl], qT[hp, pr, sl])
                St = Bh(b0, 1)
                tt(St, psS, E, ALU.mult)
                yield
                psN, psD = fix(0), fix(1)
                kwp = kwpad[h_ % 2]
                vs = slice(h_ * 128, (h_ + 1) * 128)
                pst = nxt().bitcast(BF16)
                for j in range(4):
                    sl = slice(j * 128, (j + 1) * 128)
                    tp(pst[:, j * 64:(j + 1) * 64], kT[hp, pr, sl], idb[hp, hp])
                for j in range(4):
                    act(kwp[:, j, hp], pst[:, j * 64:(j + 1) * 64], AF.Identity, scale=Gc[:, 32 + j * 8 + h_: 32 + j * 8 + h_ + 1])
                yield
                for j in range(4):
                    sl = slice(j * 128, (j + 1) * 128)
                    mm(psN[:, sl], vtm[:, j, vs], St[:, sl], st=True, sp=False)
                    mm(psN[:, sl], mlSb[hp, pr, 0:128], Qi[hp, sl], st=False, sp=True)
                    mm(psD[:, sl], ones, St[:, sl], st=True, sp=False)
                    mm(psD[:, sl], mlSb[hp, pr, 128:256], Qi[hp, sl], st=False, sp=True)
                    psc = nxt()
                    mm(psc[:, 0:128], kwp[:, j, :], vtm[:, j, vs])
                    mm(psc[:, 128:256], kwp[:, j, :], ones)
                    stt(mlS32[hp, pr, :], mlS32[hp, pr, :], csc[hp, j:j + 1], psc[hp, 0:256], ALU.mult, ALU.add)
                    cp(mlSb[hp, pr, :], mlS32[hp, pr, :], "act")
                    yield
                nsq = Bh(b0 + 1, 0)
                act(nsq, psN, AF.Square)
                psq_ = nxt()
                mm(psq_, ones, nsq)
                den2 = Ft(f0 + 2)
                act(den2, psD, AF.Square, scale=1e-3)
                tt(den2, den2, emt2, ALU.max)
                stt(den2, psq_, 1.0 / 128, den2, ALU.mult, ALU.add)
                yield
                act(den2, den2, AF.Ln)
                act(den2, den2, AF.Exp, scale=-0.5)
                hn = Ft(f0 + 3)
                tt(hn, psN, den2, ALU.mult)
                stt(ck(yT, h_), hn, C("cnw", h_), so[:, h_, :], ALU.mult, ALU.mult)
                yield
            return gen
        run_jobs([ml_head(h_) for h_ in range(8)])

    last_out = None
    for s_ in range(nseq):
        for g_ in range(NG):
            row0 = s_ * T + g_ * G
            first = (g_ == 0)
            xtm = Ft(30, 8).r("p (j d) -> p j d", j=4)
            for kc in range(8):
                ps = pb()
                for j in range(4):
                    tp(ps[:, j * 128:(j + 1) * 128], xtm[:, j, kc * 128:(kc + 1) * 128], idf)
                if kc % 2 == 0:
                    cp(ck(xT, kc), ps, "act")
                else:
                    cp(ck(xT, kc), ps, "dve")
            rmsnorm("gmix0")
            rwkv_stage(first)
            ret_stage(first, g_ * G)
            nrow = row0 + G
            if nrow < NTOK:
                dma(xtm, x_d[nrow:nrow + G, :].rearrange("(j p) d -> p j d", p=128), "xtm", eng="pool")
            outproj([10, 11], yT)
            mlp(0, 12, 20)
            rmsnorm("gmix1")
            mlstm_stage(first)
            outproj([34, 35], yT)
            mlp(1, 36, 44)
            fin = [Ft(1 + kc) for kc in range(8)]
            rmsnorm("gfin", out_bf=False, outs=fin)
            otm = Ft(22, 8).r("p (j d) -> p j d", j=4)
            for j in range(4):
                for hh in range(2):
                    ps = pb()
                    for q_ in range(4):
                        kc = hh * 4 + q_
                        tp(ps[:, q_ * 128:(q_ + 1) * 128], fin[kc][:, j * 128:(j + 1) * 128], idf)
                    cp(otm[:, j, hh * 512:(hh + 1) * 512], ps, "act" if hh else "dve")
            last_out = dma(out_d[row0:row0 + G, :].rearrange("(j p) d -> p j d", p=128), otm, "otm", eng="pool")
    P.require("pool", last_out)
    P.emit()
    return nc


_CACHE = {}


def kernel(**inputs):
    x = np.asarray(inputs["x"], np.float32)
    B, T, _ = x.shape
    nseq = B // NCORES
    shared = _prep_shared(inputs, T)
    key = (nseq, T)
    if key not in _CACHE:
        _CACHE[key] = build(nseq, T)
    nc = _CACHE[key]
    in_maps = []
    for c in range(NCORES):
        m = dict(shared)
        m["x"] = np.ascontiguousarray(x[c * nseq:(c + 1) * nseq].reshape(nseq * T, D))
        in_maps.append(m)
    res = run_bass_kernel_spmd(nc, in_maps, core_ids=list(range(NCORES)))
    out = np.concatenate([np.asarray(r["out"], np.float32).reshape(nseq, T, D) for r in res.results], 0)
    return out
```

```python
import contextlib
import math
import numpy as np
import ml_dtypes
import concourse.bass as bass
import concourse.mybir as mybir
from concourse.bass_utils import run_bass_kernel_spmd

F32 = mybir.dt.float32
BF16 = mybir.dt.bfloat16
AF = mybir.ActivationFunctionType
ALU = mybir.AluOpType
AX = mybir.AxisListType

ENGS = ("pe", "act", "dve", "pool", "sp")
NCORES = 8
D = 1024
G = 512
KAPPA = math.exp(-0.5)
NSLAB = 52
NSLOT = 3


class Prog:
    def __init__(self, nc):
        self.nc = nc
        self.q = {e: [] for e in ENGS}
        self.lastw = {}
        self.readers = {}
        self.seen = {e: {} for e in ENGS}
        self.ord = 0

    def op(self, eng, fn, reads=(), writes=(), dma=False):
        q = self.q[eng]
        idx = len(q)
        deps = set()
        for k in reads:
            w = self.lastw.get(k)
            if w is not None:
                deps.add(w)
            if k.startswith("ps"):
                for r in self.readers.get(k, ()):
                    if r[0] != eng:
                        deps.add(r)
        for k in writes:
            w = self.lastw.get(k)
            if w is not None:
                deps.add(w)
            for r in self.readers.get(k, ()):
                deps.add(r)
        waits = []
        seen = self.seen[eng]
        best = {}
        cands = []
        for (e2, i2) in deps:
            isdma = self.q[e2][i2]["dma"]
            if e2 == eng and eng == "pe" and not isdma:
                continue
            if isdma:
                cands.append((e2, i2))
            else:
                best[e2] = max(best.get(e2, -1), i2)
        cands += list(best.items())
        cands.sort(key=lambda d: -self.q[d[0]][d[1]]["ord"])
        for (e2, i2) in cands:
            ent = self.q[e2][i2]
            if ent["dma"]:
                if (e2, i2) in seen:
                    continue
                seen[(e2, i2)] = 1
            else:
                if seen.get(e2, -1) >= i2:
                    continue
                seen[e2] = i2
                for x, v in ent["snap"].items():
                    if isinstance(x, tuple):
                        seen[x] = 1
                    elif seen.get(x, -1) < v:
                        seen[x] = v
            ent["sig"] = True
            waits.append((e2, i2))
        self.ord += 1
        snap = dict(seen)
        snap[eng] = idx if not dma else snap.get(eng, -1)
        q.append(dict(fn=fn, waits=waits, sig=False, dma=dma, snap=snap, ord=self.ord))
        me = (eng, idx)
        for k in reads:
            self.readers.setdefault(k, []).append(me)
        for k in writes:
            self.lastw[k] = me
            self.readers[k] = []
        return me

    def require(self, eng, dep):
        self.q[eng].append(dict(fn=None, waits=[dep], sig=False, dma=False, snap={}, ord=self.ord))
        self.q[dep[0]][dep[1]]["sig"] = True

    def emit(self):
        nc = self.nc
        with contextlib.ExitStack() as st:
            sems = {e: st.enter_context(nc.semaphore("s_" + e)) for e in ENGS}
            dsems, dcnt, dassign = {}, {}, {}
            for e in ENGS:
                for i, it in enumerate(self.q[e]):
                    if it["dma"] and it["sig"]:
                        nm = it["dma"]
                        if nm not in dsems:
                            dsems[nm] = st.enter_context(nc.semaphore("d_" + nm))
                            dcnt[nm] = 0
                        dcnt[nm] += 16
                        dassign[(e, i)] = (dsems[nm], dcnt[nm])
            cnt = {}
            for e in ENGS:
                c = 0
                for i, it in enumerate(self.q[e]):
                    if it["sig"] and not it["dma"]:
                        c += 1
                        cnt[(e, i)] = c
            block = st.enter_context(nc.Block())

            def run(engname, eng):
                for i, it in enumerate(self.q[engname]):
                    for dep in it["waits"]:
                        if dep in dassign:
                            s, v = dassign[dep]
                            eng.wait_ge(s, v)
                        else:
                            eng.wait_ge(sems[dep[0]], cnt[dep])
                    if it["fn"] is None:
                        continue
                    ins = it["fn"](eng)
                    if it["sig"]:
                        if it["dma"]:
                            ins.then_inc(dassign[(engname, i)][0], 16)
                        else:
                            ins.then_inc(sems[engname], 1)

            @block.tensor
            def _(eng):
                run("pe", eng)

            @block.scalar
            def _(eng):
                run("act", eng)

            @block.vector
            def _(eng):
                run("dve", eng)

            @block.gpsimd
            def _(eng):
                run("pool", eng)

            @block.sync
            def _(eng):
                run("sp", eng)


class Tl:
    def __init__(self, ap, keys):
        self.ap, self.keys = ap, list(keys)

    def __getitem__(self, idx):
        return Tl(self.ap[idx], self.keys)

    def r(self, pat, **kw):
        return Tl(self.ap.rearrange(pat, **kw), self.keys)

    def bc(self, shape):
        return Tl(self.ap.to_broadcast(shape), self.keys)

    def bitcast(self, dt):
        return Tl(self.ap.bitcast(dt), self.keys)


def _keys(*ts):
    out = []
    for t in ts:
        if t is None or isinstance(t, (int, float)):
            continue
        if isinstance(t, Tl):
            out += t.keys
        elif isinstance(t, str):
            out.append(t)
    return out


def _a(x):
    return x.ap if isinstance(x, Tl) else x


def _col(v):
    return np.ascontiguousarray(np.asarray(v, np.float32).reshape(-1, 128).T)


def _slab_cols(W, cols):
    S = np.zeros((1024, 512), np.float32)
    idx = np.asarray(cols)
    val = idx >= 0
    S[:, np.nonzero(val)[0]] = W[:, idx[val]]
    return S.reshape(8, 128, 512).transpose(1, 0, 2).reshape(128, 4096)


CST = {}


def _cst_layout():
    names = [("gmix0", 8), ("gmlp0", 8), ("gmix1", 8), ("gmlp1", 8), ("gfin", 8), ("mu", 14),
             ("w0", 4), ("a0", 4), ("k_k", 4), ("k_a", 4), ("r_k", 4), ("ln_w", 4), ("ln_b", 4),
             ("cw0", 8), ("cw1", 8), ("cw2", 8), ("cw3", 8), ("cb", 8), ("cnw", 8), ("ib", 1), ("fb", 1)]
    o = 0
    for n, w in names:
        CST[n] = (o, w)
        o += w
    return o


NCST = _cst_layout()
CF_IDF = 0
NCF = 128
CB_ID, CB_ONES, CB_BONES, CB_MLM, CB_MS, CB_MI, CB_MST, CB_RET, CB_SEL, CB_RST = 0, 128, 256, 384, 896, 1024, 1152, 1280, 1408, 2432
NCB = 2432 + 512


def _const_tables(T):
    cf = np.zeros((128, NCF), np.float32)
    cf[:, CF_IDF:CF_IDF + 128] = np.eye(128, dtype=np.float32)
    p = np.arange(128)
    hs, s = p // 64, p % 64
    same = (hs[:, None] == hs[None, :])
    strict = same & (s[:, None] < s[None, :])
    incl = same & (s[:, None] <= s[None, :])
    strictT = same & (s[:, None] > s[None, :])
    ret = (p[:, None] <= p[None, :]).astype(np.float32)
    rst = np.ones(512, np.float32)
    rst[::64] = 0.0
    cb = np.zeros((128, NCB), np.float32)
    cb[:, CB_ID:CB_ID + 128] = np.eye(128)
    cb[:, CB_ONES:CB_ONES + 128] = 1.0
    cb[:, CB_BONES:CB_BONES + 128] = same.astype(np.float32)
    cb[:, CB_MLM:CB_MLM + 512] = np.tile((p[:, None] > p[None, :]).astype(np.float32) * 30000.0, (1, 4))
    cb[:, CB_MS:CB_MS + 128] = strict
    cb[:, CB_MI:CB_MI + 128] = incl
    cb[:, CB_MST:CB_MST + 128] = strictT
    cb[:, CB_RET:CB_RET + 128] = ret
    for j in range(8):
        cb[j, CB_SEL + j * 128: CB_SEL + (j + 1) * 128] = 1.0
    cb[:, CB_RST:CB_RST + 512] = rst[None, :]
    d = 128
    inv = (np.float32(10000.0) ** (-(np.arange(0, d, 2, dtype=np.float32) / np.float32(d)))).astype(np.float32)
    pos = np.arange(T, dtype=np.float32)
    ang = (pos[:, None] * inv[None, :]).astype(np.float32).astype(np.float64)
    cos = np.cos(ang).T
    sin = np.sin(ang).T
    cos2 = np.concatenate([cos, cos], 0)
    sin2 = np.concatenate([-sin, sin], 0)
    l1 = (np.arange(T) % 128 + 1).astype(np.float64)
    rot = np.zeros((16, 128, T), np.float32)
    for h in range(4):
        lg = math.log1p(-2.0 ** (-5.0 - h))
        dq = np.exp(lg * l1)[None, :]
        dk = np.exp(-lg * l1)[None, :] * (128.0 ** -0.5)
        rot[h * 4 + 0] = cos2 * dq
        rot[h * 4 + 1] = sin2 * dq
        rot[h * 4 + 2] = cos2 * dk
        rot[h * 4 + 3] = sin2 * dk
    return cf, cb.astype(ml_dtypes.bfloat16), rot


def _prep_shared(inp, T):
    f = np.float32
    Wab = np.asarray(inp["ab_w_in"][0], f)
    slabs = []
    ar = np.arange
    slabs.append(_slab_cols(Wab, ar(0, 512)))
    slabs.append(_slab_cols(Wab, ar(512, 1024)))
    slabs.append(_slab_cols(Wab, ar(1024, 1536)))
    slabs.append(_slab_cols(Wab, list(ar(1536, 1792)) + [-1] * 256))
    rb = 1792
    dd = ar(128)
    sw = np.where(dd < 64, dd + 64, dd - 64)
    qcols = np.concatenate([rb + h * 128 + dd for h in range(4)])
    qsw = np.concatenate([rb + h * 128 + sw for h in range(4)])
    slabs.append(_slab_cols(Wab, qcols))
    slabs.append(_slab_cols(Wab, qsw))
    slabs.append(_slab_cols(Wab, qcols + 512))
    slabs.append(_slab_cols(Wab, qsw + 512))
    slabs.append(_slab_cols(Wab, ar(rb + 1024, rb + 1536)))
    slabs.append(_slab_cols(Wab, ar(rb + 1536, rb + 2048)))
    Wo = np.asarray(inp["ab_w_out"][0], f)
    slabs += [_slab_cols(Wo, ar(0, 512)), _slab_cols(Wo, ar(512, 1024))]

    def mlp_slabs(l):
        w1 = np.asarray(inp["mlp_w1"][l], f)
        w2 = np.asarray(inp["mlp_w2"][l], f)
        out = [_slab_cols(w1, ar(j * 512, (j + 1) * 512)) for j in range(8)]
        w2r = w2.reshape(32, 128, 8, 128).transpose(1, 0, 2, 3)
        out += [np.ascontiguousarray(w2r[:, :, dj, :]).reshape(128, 4096) for dj in range(8)]
        return out
    slabs += mlp_slabs(0)
    Wc = np.asarray(inp["c_w_in"][0], f)
    slabs += [_slab_cols(Wc, ar(j * 512, (j + 1) * 512)) for j in range(6)]
    Wco = np.asarray(inp["c_w_out"][0], f)
    slabs += [_slab_cols(Wco, ar(0, 512)), _slab_cols(Wco, ar(512, 1024))]
    slabs += mlp_slabs(1)
    wsl = np.stack(slabs).astype(f)
    assert wsl.shape[0] == NSLAB
    cst = np.zeros((128, NCST), f)

    def put(n, v):
        o, w = CST[n]
        cst[:, o:o + w] = _col(v)
    put("gmix0", inp["norm_mix_g"][0]); put("gmix1", inp["norm_mix_g"][1])
    put("gmlp0", inp["norm_mlp_g"][0]); put("gmlp1", inp["norm_mlp_g"][1])
    put("gfin", inp["norm_final_g"])
    put("mu", inp["rwkv_mu"][0]); put("w0", inp["rwkv_w0"][0]); put("a0", inp["rwkv_a0"][0])
    put("k_k", inp["rwkv_k_k"][0]); put("k_a", inp["rwkv_k_a"][0])
    put("r_k", np.asarray(inp["rwkv_r_k"][0]).reshape(-1))
    put("ln_w", inp["rwkv_ln_w"][0]); put("ln_b", inp["rwkv_ln_b"][0])
    cw = np.asarray(inp["c_conv_w"][0], f)
    for j in range(4):
        put("cw%d" % j, cw[j])
    put("cb", inp["c_conv_b"][0]); put("cnw", inp["c_norm_w"][0])
    cst[0:8, CST["ib"][0]] = np.asarray(inp["c_i_bias"][0], f)
    cst[0:8, CST["fb"][0]] = np.asarray(inp["c_f_bias"][0], f)
    lora = np.zeros((128, 3, 512), f)
    lora[0:64, 0] = np.asarray(inp["rwkv_w_up"][0], f)
    lora[64:128, 1] = np.asarray(inp["rwkv_a_up"][0], f)
    lora[:, 2] = np.asarray(inp["rwkv_g_up"][0], f)
    gw = np.ascontiguousarray(Wc[:, 3072:3088].reshape(8, 128, 16).transpose(1, 0, 2))
    cf, cb, rot = _const_tables(T)
    return dict(wsl=wsl, cst=cst, lora=lora.reshape(128, 1536), gw=gw.reshape(128, 128), cf=cf, cb=cb, rot=rot)


def build(nseq, T, dbg=None):
    nc = bass.Bass("TRN2", target_bir_lowering=False)
    P = Prog(nc)
    NG = T // G
    NTOK = nseq * T

    def din(name, shape, dt=F32):
        return nc.dram_tensor(name, shape, dt, kind="ExternalInput").ap()
    x_d = din("x", [NTOK, D])
    wsl_d = din("wsl", [NSLAB, 128, 4096])
    cst_d = din("cst", [128, NCST])
    lora_d = din("lora", [128, 1536])
    gw_d = din("gw", [128, 128])
    cf_d = din("cf", [128, NCF])
    cb_d = din("cb", [128, NCB], BF16)
    rot_d = din("rot", [16, 128, T])
    out_d = nc.dram_tensor("out", [NTOK, D], F32, kind="ExternalOutput").ap()
    wbf_d = nc.dram_tensor("wbf", [NSLAB, 128, 4096], BF16, kind="Internal").ap()
    dbg_d = None
    if dbg is not None:
        dbg_d = nc.dram_tensor("dbg", [128, dbg], F32, kind="ExternalOutput").ap()

    def sb(name, shape, dt=F32):
        return Tl(nc.alloc_sbuf_tensor("sb_" + name, shape, dt).ap(), [name])

    NF, NB = 38, 25
    FP = nc.alloc_sbuf_tensor("FP", [128, NF * 512], F32).ap()
    BP = nc.alloc_sbuf_tensor("BP", [128, NB * 1024], BF16).ap()
    PSA = nc.alloc_psum_tensor("PS", [128, 4096], F32).ap()

    def Ft(i, n=1):
        return Tl(FP[:, i * 512:(i + n) * 512], ["F%d" % j for j in range(i, i + n)])

    def Bt(i, n=1):
        return Tl(BP[:, i * 1024:(i + n) * 1024], ["B%d" % j for j in range(i, i + n)])

    def Bh(i, half):
        return Tl(BP[:, i * 1024 + half * 512: i * 1024 + half * 512 + 512], ["B%d" % i])

    def PSb(b, n=1):
        return Tl(PSA[:, b * 512:(b + n) * 512], ["ps%d" % j for j in range(b, b + n)])

    psctr = [0]

    def pb(n=1):
        b = psctr[0]
        if b + n > 8:
            b = 0
        psctr[0] = (b + n) % 8
        return PSb(b, n)

    def sbk(name, shape, dt=F32):
        return Tl(nc.alloc_sbuf_tensor("sb_" + name, shape, dt).ap(), ["%s%d" % (name, i) for i in range(shape[1])])

    def ck(t_, i):
        return Tl(t_.ap[:, i, :], [t_.keys[i]])
    rwS = sbk("rwS", [128, 4, 128], BF16)
    xT = sbk("xT", [128, 8, 512])
    hT = sbk("hT", [128, 8, 512], BF16)
    yT = sbk("yT", [128, 8, 512], BF16)
    ring = [sb("ring%d" % i, [128, 4096], BF16) for i in range(NSLOT)]
    cst = sb("cst", [128, NCST])
    cst2 = sb("cst2", [128, 24])
    cf = sb("cf", [128, NCF])
    cb = sb("cb", [128, NCB], BF16)
    lora = sb("lora", [128, 1536], BF16)
    gw = sb("gw", [128, 128], BF16)
    rwcar = sb("rwcar", [128, 14])
    rtS32 = sb("rtS32", [128, 4, 128])
    rtSb = sb("rtSb", [128, 4, 128], BF16)
    mlS32 = sb("mlS32", [128, 4, 256])
    mlSb = sb("mlSb", [128, 4, 256], BF16)
    mlcar = sb("mlcar", [128, 8, 3])
    mlrow = sb("mlrow", [8, 2])
    kwpad = [sb("kwpad%d" % i, [128, 4, 128], BF16) for i in range(2)]
    small = sb("small", [128, 64])
    wl_t = [sb("wl%d" % i, [128, 8]) for i in range(2)]
    sb_gc = sb("gc", [128, 64])

    def C(name, i=0, n=1):
        o, w = CST[name]
        return cst[:, o + i:o + i + n]

    def mm(o, l, r, st=True, sp=True):
        P.op("pe", lambda e: e.matmul(o.ap, lhsT=l.ap, rhs=r.ap, start=st, stop=sp), reads=_keys(l, r), writes=_keys(o))

    def tp(o, i, ident):
        P.op("pe", lambda e: e.transpose(o.ap, i.ap, ident.ap), reads=_keys(i, ident), writes=_keys(o))

    def act(o, i, f, scale=None, bias=None):
        kw = {}
        if scale is not None:
            kw["scale"] = _a(scale)
        if bias is not None:
            kw["bias"] = _a(bias)
        P.op("act", lambda e: e.activation(out=o.ap, in_=i.ap, func=f, **kw), reads=_keys(i, scale, bias), writes=_keys(o))

    def tt(o, a, b, op, eng="dve"):
        P.op(eng, lambda e: e.tensor_tensor(out=o.ap, in0=a.ap, in1=b.ap, op=op), reads=_keys(a, b), writes=_keys(o))

    def ts(o, a, s1, op0, s2=None, op1=None, eng="dve"):
        if op1 is None:
            P.op(eng, lambda e: e.tensor_scalar(out=o.ap, in0=a.ap, scalar1=_a(s1), scalar2=None, op0=op0),
                 reads=_keys(a, s1), writes=_keys(o))
        else:
            P.op(eng, lambda e: e.tensor_scalar(out=o.ap, in0=a.ap, scalar1=_a(s1), scalar2=_a(s2), op0=op0, op1=op1),
                 reads=_keys(a, s1, s2), writes=_keys(o))

    def stt(o, a, s, b, op0, op1):
        P.op("dve", lambda e: e.scalar_tensor_tensor(out=o.ap, in0=a.ap, scalar=_a(s), in1=b.ap, op0=op0, op1=op1),
             reads=_keys(a, s, b), writes=_keys(o))

    def scan(o, d0, d1, init, op0, op1):
        P.op("dve", lambda e: e.tensor_tensor_scan(out=o.ap, data0=d0.ap, data1=d1.ap, initial=_a(init), op0=op0, op1=op1),
             reads=_keys(d0, d1, init), writes=_keys(o))

    def red(o, i, op):
        P.op("dve", lambda e: e.tensor_reduce(out=o.ap, in_=i.ap, axis=AX.X, op=op), reads=_keys(i), writes=_keys(o))

    def cp(o, i, eng="dve"):
        if eng == "act":
            P.op("act", lambda e: e.copy(out=o.ap, in_=i.ap), reads=_keys(i), writes=_keys(o))
        else:
            P.op(eng, lambda e: e.tensor_copy(out=o.ap, in_=i.ap), reads=_keys(i), writes=_keys(o))

    def mset(o, v, eng="pool"):
        P.op(eng, lambda e: e.memset(o.ap, v), writes=_keys(o))

    def dma(o, i, sem, eng="sp", rk=(), wk=()):
        return P.op(eng, lambda e: e.dma_start(out=_a(o), in_=_a(i)), reads=_keys(i) + list(rk), writes=_keys(o) + list(wk), dma=sem)

    idf = cf[:, CF_IDF:CF_IDF + 128]
    idb = cb[:, CB_ID:CB_ID + 128]
    ones = cb[:, CB_ONES:CB_ONES + 128]
    bones = cb[:, CB_BONES:CB_BONES + 128]
    mlmask = cb[:, CB_MLM:CB_MLM + 512]

    def mask4(off):
        m = cb[:, off:off + 128]
        return Tl(m.ap.unsqueeze(1).to_broadcast([128, 4, 128]), m.keys)

    def run_jobs(jobs, width=2):
        jobs = list(jobs)
        active = []
        nxt = 0
        free = list(range(width))
        while active or nxt < len(jobs):
            while free and nxt < len(jobs):
                sl_ = free.pop(0)
                active.append((sl_, jobs[nxt](sl_)))
                nxt += 1
            for ent in list(active):
                try:
                    next(ent[1])
                except StopIteration:
                    active.remove(ent)
                    free.append(ent[0])

    def slot_ps(slot):
        ctr = [0]

        def rot(n=1):
            b = ctr[0] % 4
            if b + n > 4:
                b = 0
            ctr[0] = b + n
            return PSb(4 * slot + b, n)
        return rot, (lambda i, n=1: PSb(4 * slot + i, n))

    dma(cst, cst_d, "cst"); dma(cf, cf_d, "cf"); dma(cb, cb_d, "cb")
    l32 = Ft(0, 3)
    dma(l32, lora_d, "l32")
    cp(lora, l32)
    g32 = Ft(3)[:, 0:128]
    dma(g32, gw_d, "g32")
    cp(gw, g32)
    o, w = CST["mu"]
    ts(cst2[:, 0:14], cst[:, o:o + 14], -1.0, ALU.mult, 1.0, ALU.add)
    o, w = CST["k_a"]
    ts(cst2[:, 14:18], cst[:, o:o + 4], -1.0, ALU.mult, 1.0, ALU.add)
    ts(cst2[:, 18:19], C("ib"), 1.0 / 15.0, ALU.mult)
    ts(cst2[:, 19:20], C("fb"), 1.0 / 15.0, ALU.mult)
    for t_ in kwpad:
        mset(t_, 0.0)
    xtm0 = Ft(30, 8).r("p (j d) -> p j d", j=4)
    dma(xtm0, x_d[0:G, :].rearrange("(j p) d -> p j d", p=128), "xtm", eng="pool")
    for s_ in range(NSLAB):
        dma(wbf_d[s_], wsl_d[s_], "wc%d" % s_, eng="pool", wk=["wbf%d" % s_])

    ringctr = [0]

    def wload(slab):
        t_ = ring[ringctr[0] % NSLOT]
        ringctr[0] += 1
        dma(t_, wbf_d[slab], t_.keys[0], rk=["wbf%d" % slab])
        return t_

    def rmsnorm(gname, out_bf=True, outs=None):
        sq = Bt(0, 4)
        act(sq, xT.r("p a b -> p (a b)"), AF.Square)
        ps = pb()
        for kc in range(8):
            mm(ps, ones, sq[:, kc * 512:(kc + 1) * 512], st=(kc == 0), sp=(kc == 7))
        rstd = Ft(0)
        act(rstd, ps, AF.Ln, scale=1.0 / D, bias=1e-6)
        act(rstd, rstd, AF.Exp, scale=-0.5)
        for kc in range(8):
            dst = ck(hT, kc) if out_bf else outs[kc]
            stt(dst, ck(xT, kc), C(gname, kc), rstd, ALU.mult, ALU.mult)

    def proj_fm(slab, chunks, evac):
        wt = wload(slab).r("p (k c) -> p k c", k=8)
        for jc in chunks:
            ps = pb()
            for kc in range(8):
                mm(ps, wt[:, kc, jc * 128:(jc + 1) * 128], ck(hT, kc), st=(kc == 0), sp=(kc == 7))
            evac(jc, ps)

    def proj_tm(slab, evac):
        wt = wload(slab).r("p (k c) -> p k c", k=8)
        for j in range(4):
            ps = pb()
            for kc in range(8):
                mm(ps, ck(hT, kc)[:, j * 128:(j + 1) * 128], wt[:, kc, :], st=(kc == 0), sp=(kc == 7))
            evac(j, ps)

    def outproj(slabs, src):
        for si, slab in enumerate(slabs):
            wt = wload(slab).r("p (k c) -> p k c", k=8)
            for jc in range(4):
                ps = pb()
                for kc in range(8):
                    mm(ps, wt[:, kc, jc * 128:(jc + 1) * 128], ck(src, kc), st=(kc == 0), sp=(kc == 7))
                dc = si * 4 + jc
                tt(ck(xT, dc), ck(xT, dc), ps, ALU.add)

    def mlp(layer, w1s, w2s):
        rmsnorm("gmlp%d" % layer)
        hid = Bt(4, 16).r("p (f t) -> p f t", f=32)
        for j in range(8):
            wt = wload(w1s + j).r("p (k c) -> p k c", k=8)
            for jc in range(4):
                ps = pb()
                for kc in range(8):
                    mm(ps, wt[:, kc, jc * 128:(jc + 1) * 128], ck(hT, kc), st=(kc == 0), sp=(kc == 7))
                f_ = j * 4 + jc
                tmp = Ft(1 + (f_ % 2))
                act(tmp, ps, AF.Relu)
                tt(hid[:, f_, :], tmp, tmp, ALU.mult, eng="pool")
        for dj in range(8):
            wt = wload(w2s + dj).r("p (f c) -> p f c", f=32)
            ps = pb()
            for f_ in range(32):
                mm(ps, wt[:, f_, :], hid[:, f_, :], st=(f_ == 0), sp=(f_ == 31))
            tt(ck(xT, dj), ck(xT, dj), ps, ALU.add)

    def rwkv_stage(first):
        if first:
            mset(rwcar, 0.0)
            mset(rwS, 0.0)
        rT = [Ft(1 + i) for i in range(4)]
        kT = [Ft(5 + i) for i in range(4)]
        vT = [Ft(9 + i) for i in range(4)]
        waT, gloT = Ft(13), Ft(14)

        def lerp(col, ps, dst):
            tmpl = Ft(19) if col % 2 == 0 else Ft(30)
            act(tmpl, ps, AF.Identity, scale=cst2[:, col:col + 1])
            stt(dst[:, 1:512], ps[:, 0:511], C("mu", col), tmpl[:, 1:512], ALU.mult, ALU.add)
            stt(dst[:, 0:1], rwcar[:, col:col + 1], C("mu", col), tmpl[:, 0:1], ALU.mult, ALU.add)
            cp(rwcar[:, col:col + 1], ps[:, 511:512], "act")
        proj_fm(0, range(4), lambda jc, ps: lerp(jc, ps, rT[jc]))
        proj_fm(1, range(4), lambda jc, ps: lerp(4 + jc, ps, kT[jc]))
        proj_fm(2, range(4), lambda jc, ps: lerp(8 + jc, ps, vT[jc]))
        proj_fm(3, range(2), lambda jc, ps: lerp(12 + jc, ps, waT if jc == 0 else gloT))
        lo = Bt(0)
        act(lo[0:64, 0:512], waT[0:64, :], AF.Tanh)
        act(lo[64:128, 0:512], waT[64:128, :], AF.Identity)
        act(lo[:, 512:1024], gloT, AF.Sigmoid)
        lora3 = lora.r("p (a c) -> p a c", a=3)
        BDN = ("At", "Rt", "Bt", "Kt", "Vt", "BWt", "KWt")

        def pair_job(p_):
            def gen(slot):
                rot, fix = slot_ps(slot)
                fb = 13 + 12 * slot
                yfm, gTp, t1, t2, t3, t4, kkn, bb, bon, Yl, Ys, sq = [Ft(fb + i) for i in range(12)]
                lw, av, cr = Yl, Ys, sq
                b0 = 1 + 12 * slot
                kk2, rk = Bh(b0, 0), Bh(b0, 1)
                bd = {n: Bh(b0 + 1 + i // 2, i % 2).r("p (c t) -> p c t", c=4) for i, n in enumerate(BDN)}
                Ark = Bh(b0 + 4, 1)
                for n in BDN:
                    mset(bd[n], 0.0, eng="pool")
                cs_ = slice(p_ * 128, (p_ + 1) * 128)
                r_, k_, v_ = rT[p_], kT[p_], vT[p_]
                ps = rot()
                mm(ps, lora3[0:64, 0, cs_], lo[0:64, 0:512])
                act(lw, ps, AF.Sigmoid, bias=C("w0", p_))
                yield
                ps = rot()
                mm(ps, lora3[64:128, 1, cs_], lo[64:128, 0:512])
                act(av, ps, AF.Sigmoid, bias=C("a0", p_))
                yield
                ps = rot()
                mm(ps, lora3[:, 2, cs_], lo[:, 512:1024])
                cp(gTp, ps, "act")
                act(kk2, k_, AF.Square, scale=C("k_k", p_))
                yield
                ps = rot()
                mm(ps, bones, kk2)
                rs = bon
                ts(rs, ps, 1e-24, ALU.max)
                yield
                act(rs, rs, AF.Ln)
                scan(cr, cb[:, CB_RST:CB_RST + 512], lw, 0.0, ALU.mult, ALU.add)
                yield
                act(rs, rs, AF.Exp, scale=-0.5)
                tt(t1, cr, lw, ALU.subtract)
                yield
                stt(kkn, k_, C("k_k", p_), rs, ALU.mult, ALU.mult)
                act(t1, t1, AF.Exp, scale=-KAPPA)
                yield
                t2p = yfm
                ts(t2p, av, C("k_a", p_), ALU.mult, cst2[:, 14 + p_:15 + p_], ALU.add)
                act(t2, cr, AF.Exp, scale=KAPPA)
                yield
                tt(k_, k_, t2p, ALU.mult)
                act(t3, cr, AF.Exp, scale=-KAPPA)
                yield
                tt(bb, kkn, av, ALU.mult)
                cr3 = cr.r("p (c t) -> p c t", t=64)
                tt(t4.r("p (c t) -> p c t", t=64), cr3, cr3[:, :, 63:64].bc([128, 8, 64]), ALU.subtract)
                yield
                act(t4, t4, AF.Exp, scale=KAPPA)
                WL = small[:, 56 + 4 * 0: 56 + 4 * 0] if False else wl_t[slot]
                act(WL, cr3[:, :, 63], AF.Exp, scale=-KAPPA)
                stt(rk, r_, C("r_k", p_), k_, ALU.mult, ALU.mult)
                yield
                ps = rot()
                mm(ps, bones, rk)
                tt(bon, ps, v_, ALU.mult)
                yield
                for hf in range(2):
                    c0 = hf * 4
                    for h_ in range(2):
                        hp = slice(h_ * 64, (h_ + 1) * 64)
                        fs = slice(h_ * 64, (h_ + 1) * 64)

                        def v3(t_):
                            return t_[hp, c0 * 64:(c0 + 4) * 64].r("p (c t) -> p c t", t=64)
                        stt(bd["At"][hp, :, fs], v3(kkn), -1.0, v3(t1), ALU.mult, ALU.mult)
                        tt(bd["Rt"][hp, :, fs], v3(r_), v3(t3), ALU.mult)
                        tt(bd["Bt"][hp, :, fs], v3(bb), v3(t2), ALU.mult)
                        yield
                        tt(bd["Kt"][hp, :, fs], v3(k_), v3(t2), ALU.mult)
                        cp(bd["Vt"][hp, :, fs], v3(v_), "act")
                        tt(bd["BWt"][hp, :, fs], v3(bb), v3(t4), ALU.mult, eng="pool")
                        tt(bd["KWt"][hp, :, fs], v3(k_), v3(t4), ALU.mult)
                        yield
                    Vtm, BWtm, KWtm = Bh(b0 + 5, 0), Bh(b0 + 5, 1), Bh(b0 + 6, 0)
                    X = Bt(b0 + 7).r("p (c x) -> p c x", c=4)
                    for kind, dst in (("Vt", Vtm), ("BWt", BWtm), ("KWt", KWtm), ("At", None)):
                        psb_ = rot().bitcast(BF16)[:, 0:512]
                        for c in range(4):
                            tp(psb_[:, c * 128:(c + 1) * 128], bd[kind][:, c, :], idb)
                        if dst is not None:
                            cp(dst, psb_, "act")
                        else:
                            cp(X[:, :, 0:128], psb_.r("p (c t) -> p c t", c=4), "act")
                        yield
                    Mj, Nj = Bh(b0 + 8, 0), Bh(b0 + 8, 1)
                    Aak, Arb = Bh(b0 + 10, 0), Bh(b0 + 10, 1)
                    specs = (("Bt", "At", Mj, CB_MS), ("At", "Bt", Nj, CB_MST), ("Kt", "At", Aak, CB_MS),
                             ("Bt", "Rt", Arb, CB_MI), ("Kt", "Rt", Ark, CB_MI))
                    for (l_, r2_, dst, mo) in specs:
                        ps = rot()
                        for c in range(4):
                            mm(ps[:, c * 128:(c + 1) * 128], bd[l_][:, c, :], bd[r2_][:, c, :])
                        tt(dst.r("p (c t) -> p c t", c=4), ps.r("p (c t) -> p c t", c=4), mask4(mo), ALU.mult)
                        yield
                    ps = rot()
                    for c in range(4):
                        mm(ps[:, c * 128:(c + 1) * 128], Aak[:, c * 128:(c + 1) * 128], Vtm[:, c * 128:(c + 1) * 128])
                    cp(X[:, :, 128:256], ps.r("p (c t) -> p c t", c=4), "act")
                    yield
                    Xf = X.r("p c x -> p (c x)")
                    for j in range(6):
                        ps2 = fix(0, 2)
                        for c in range(4):
                            mm(ps2[:, c * 256:(c + 1) * 256], idb, X[:, c, :], st=True, sp=False)
                            mm(ps2[:, c * 256:(c + 1) * 256], Mj[:, c * 128:(c + 1) * 128], X[:, c, :], st=False, sp=True)
                        if j < 5:
                            psm, psn = fix(2), fix(3)
                            for c in range(4):
                                sl = slice(c * 128, (c + 1) * 128)
                                mm(psm[:, sl], Nj[:, sl], Mj[:, sl])
                                mm(psn[:, sl], Mj[:, sl], Nj[:, sl])
                        cp(Xf, ps2, "act")
                        if j < 5:
                            Mn, Nn = (Bh(b0 + 9, 0), Bh(b0 + 9, 1)) if j % 2 == 0 else (Bh(b0 + 8, 0), Bh(b0 + 8, 1))
                            cp(Mn, psm, "dve")
                            cp(Nn, psn, "dve")
                            Mj, Nj = Mn, Nn
                        yield
                    GT, QT = Bh(b0 + 11, 0), Bh(b0 + 11, 1)
                    Hb = Bh(b0 + 8, 0)
                    ps = rot()
                    for c in range(4):
                        mm(ps[:, c * 128:(c + 1) * 128], X[:, c, 0:128], BWtm[:, c * 128:(c + 1) * 128])
                    for c in range(4):
                        sl = slice(c * 128, (c + 1) * 128)
                        stt(GT[:, sl], idf, WL[:, c0 + c:c0 + c + 1], ps[:, sl], ALU.mult, ALU.add)
                    yield
                    ps = rot()
                    for c in range(4):
                        sl = slice(c * 128, (c + 1) * 128)
                        mm(ps[:, sl], BWtm[:, sl], X[:, c, 128:256], st=True, sp=False)
                        mm(ps[:, sl], KWtm[:, sl], Vtm[:, sl], st=False, sp=True)
                    cp(Hb, ps, "act")
                    yield
                    ps = rot()
                    for c in range(4):
                        sl = slice(c * 128, (c + 1) * 128)
                        mm(ps[:, sl], idb, bd["Rt"][:, c, :], st=True, sp=False)
                        mm(ps[:, sl], X[:, c, 0:128], Arb[:, sl], st=False, sp=True)
                    cp(QT, ps, "act")
                    yield
                    ps = rot()
                    for c in range(4):
                        sl = slice(c * 128, (c + 1) * 128)
                        mm(ps[:, sl], Arb[:, sl], X[:, c, 128:256], st=True, sp=False)
                        mm(ps[:, sl], Ark[:, sl], Vtm[:, sl], st=False, sp=True)
                    cp(Yl, ps, "act")
                    yield
                    psy = fix(0)
                    for c in range(4):
                        sl = slice(c * 128, (c + 1) * 128)
                        mm(psy[:, sl], QT[:, sl], ck(rwS, p_))
                        pss = fix(1 + c % 2)
                        mm(pss[:, 0:128], GT[:, sl], ck(rwS, p_), st=True, sp=False)
                        mm(pss[:, 0:128], idb, Hb[:, sl], st=False, sp=True)
                        cp(ck(rwS, p_), pss[:, 0:128], "act")
                        yield
                    tt(Ys, psy, Yl, ALU.add)
                    st_ = small[:, 8 + 24 * slot:32 + 24 * slot]
                    Ys3 = Ys.r("p (c v) -> p c v", c=4)
                    red(st_[:, 0:4], Ys3, ALU.add)
                    act(sq, Ys, AF.Square)
                    red(st_[:, 4:8], sq.r("p (c v) -> p c v", c=4), ALU.add)
                    yield
                    ts(st_[:, 8:12], st_[:, 0:4], 1.0 / 64, ALU.mult)
                    tt(st_[:, 12:16], st_[:, 8:12], st_[:, 8:12], ALU.mult)
                    stt(st_[:, 16:20], st_[:, 4:8], 1.0 / 64, st_[:, 12:16], ALU.mult, ALU.subtract)
                    act(st_[:, 16:20], st_[:, 16:20], AF.Ln, bias=64e-5)
                    act(st_[:, 20:24], st_[:, 16:20], AF.Exp, scale=-0.5)
                    yield
                    for c in range(4):
                        sl = slice(c * 128, (c + 1) * 128)
                        ts(sq[:, sl], Ys[:, sl], st_[:, 8 + c:9 + c], ALU.subtract, st_[:, 20 + c:21 + c], ALU.mult)
                    ps = fix(3)
                    for c in range(4):
                        sl = slice(c * 128, (c + 1) * 128)
                        tp(ps[:, sl], sq[:, sl], idf)
                    ps3 = ps.r("p (c t) -> p c t", c=4)
                    for h_ in range(2):
                        hp = slice(h_ * 64, (h_ + 1) * 64)
                        act(yfm[hp, hf * 256:(hf + 1) * 256].r("p (c t) -> p c t", c=4), ps3[hp, :, h_ * 64:(h_ + 1) * 64],
                            AF.Identity, scale=C("ln_w", p_)[hp, :], bias=C("ln_b", p_)[hp, :])
                    yield
                tt(yfm, yfm, bon, ALU.add)
                tt(ck(yT, p_), yfm, gTp, ALU.mult)
                yield
            return gen
        run_jobs([pair_job(p_) for p_ in range(4)])

    def ret_stage(first, t0):
        if first:
            mset(rtS32, 0.0)
            mset(rtSb, 0.0)
        qh = Bt(0, 2).r("p (h t) -> p h t", h=4)
        kh = Bt(2, 2).r("p (h t) -> p h t", h=4)
        vtm = Bt(4, 2).r("p (j c) -> p j c", j=4)
        ktm = Bt(6, 2).r("p (j c) -> p j c", j=4)
        sg = Bt(8, 2).r("p (h t) -> p h t", h=4)
        psq = {}

        def rot(h_, which, psa, psb_, dst):
            tabs = []
            for kd in range(2):
                tb = Ft((1 if h_ % 2 == 0 else 15) + kd + 2 * which)
                dma(tb, rot_d[h_ * 4 + which * 2 + kd][:, t0:t0 + G], tb.keys[0], eng="pool")
                tabs.append(tb)
            u1, u2 = (Ft(5), Ft(6)) if h_ % 2 == 0 else (Ft(13), Ft(14))
            tt(u1, psa, tabs[0], ALU.mult)
            tt(u2, psb_, tabs[1], ALU.mult)
            tt(dst, u1, u2, ALU.add, eng="pool")
        for which, (sa_, sb_, dst) in enumerate(((4, 5, qh), (6, 7, kh))):
            wa_ = wload(sa_).r("p (k c) -> p k c", k=8)
            wb_ = wload(sb_).r("p (k c) -> p k c", k=8)
            for h_ in range(4):
                pa, pb_ = pb(), pb()
                for kc in range(8):
                    mm(pa, wa_[:, kc, h_ * 128:(h_ + 1) * 128], ck(hT, kc), st=(kc == 0), sp=(kc == 7))
                for kc in range(8):
                    mm(pb_, wb_[:, kc, h_ * 128:(h_ + 1) * 128], ck(hT, kc), st=(kc == 0), sp=(kc == 7))
                rot(h_, which, pa, pb_, dst[:, h_, :])
        proj_tm(8, lambda j, ps: cp(vtm[:, j, :], ps, "act"))
        proj_fm(9, range(4), lambda jc, ps: act(sg[:, jc, :], ps, AF.Silu))
        for h_ in range(4):
            psb_ = pb().bitcast(BF16)[:, 0:512]
            for j in range(4):
                tp(psb_[:, j * 128:(j + 1) * 128], kh[:, h_, j * 128:(j + 1) * 128], idb)
            cp(ktm[:, :, h_ * 128:(h_ + 1) * 128], psb_.r("p (j d) -> p j d", j=4), "act")
        def ret_head(h_):
            def gen(slot):
                rot, fix = slot_ps(slot)
                gl = math.exp(128.0 * math.log1p(-2.0 ** (-5.0 - h_)))
                hs_ = slice(h_ * 128, (h_ + 1) * 128)
                ps = fix(3)
                for j in range(4):
                    sl = slice(j * 128, (j + 1) * 128)
                    mm(ps[:, sl], kh[:, h_, sl], qh[:, h_, sl])
                ST, osq = Bh(10 + slot, 0), Bh(10 + slot, 1)
                tt(ST.r("p (c t) -> p c t", c=4), ps.r("p (c t) -> p c t", c=4), mask4(CB_RET), ALU.mult)
                yield
                pso = fix(0)
                psr = fix(1)
                for j in range(4):
                    sl = slice(j * 128, (j + 1) * 128)
                    mm(psr[:, sl], ktm[:, j, hs_], vtm[:, j, hs_])
                yield
                for j in range(4):
                    sl = slice(j * 128, (j + 1) * 128)
                    mm(pso[:, sl], vtm[:, j, hs_], ST[:, sl], st=True, sp=False)
                    mm(pso[:, sl], rtSb[:, h_, :], qh[:, h_, sl], st=False, sp=True)
                    tmp = Ft(7 + 3 * slot)[:, 0:128]
                    tt(tmp, psr[:, sl], rtS32[:, h_, :], ALU.add)
                    act(rtS32[:, h_, :], tmp, AF.Identity, scale=gl)
                    act(rtSb[:, h_, :], tmp, AF.Identity, scale=gl)
                    yield
                o32 = Ft(8 + 3 * slot)
                cp(o32, pso, "act")
                act(osq, pso, AF.Square)
                psm = fix(3)
                mm(psm, ones, osq)
                rs = Ft(9 + 3 * slot)
                act(rs, psm, AF.Ln, scale=1.0 / 128, bias=1e-6)
                act(rs, rs, AF.Exp, scale=-0.5)
                yield
                tt(o32, o32, rs, ALU.mult)
                tt(ck(yT, 4 + h_), o32, sg[:, h_, :], ALU.mult)
                yield
            return gen
        run_jobs([ret_head(h_) for h_ in range(4)])

    def mlstm_stage(first):
        if first:
            for t_ in (mlS32, mlSb, mlcar, mlrow):
                mset(t_, 0.0)
        qT = Bt(0, 2).r("p (c t) -> p c t", c=4)
        kT = Bt(2, 2).r("p (c t) -> p c t", c=4)
        vtm = Bt(4, 4).r("p (j c) -> p j c", j=4)
        so = Bt(8, 4).r("p (h t) -> p h t", h=8)
        def conv(ch, ps, dst):
            xe = (Ft(1, 2) if ch % 2 == 0 else Ft(22, 2))[:, 0:515]
            cp(xe[:, 3:515], ps, "act")
            cp(xe[:, 0:3], mlcar[:, ch, :], "act")
            acc = Ft(3) if ch % 2 == 0 else Ft(24)
            ts(acc, xe[:, 0:512], C("cw0", ch), ALU.mult, C("cb", ch), ALU.add)
            for j in range(1, 4):
                stt(acc, xe[:, j:j + 512], C("cw%d" % j, ch), acc, ALU.mult, ALU.add)
            cp(mlcar[:, ch, :], xe[:, 512:515], "act")
            act(dst, acc, AF.Silu)
        gw3 = gw.r("p (k c) -> p k c", k=8)
        psi, psf = pb(), pb()
        for kc in range(8):
            mm(psi[0:8, :], gw3[:, kc, 0:8], ck(hT, kc), st=(kc == 0), sp=(kc == 7))
        for kc in range(8):
            mm(psf[0:8, :], gw3[:, kc, 8:16], ck(hT, kc), st=(kc == 0), sp=(kc == 7))
        R = [Ft(4 + i)[0:8, :] for i in range(8)]
        li, lf, Bc, Gs, Mx, em, inter, wr = R
        act(li, psi[0:8, :], AF.Tanh, scale=1.0 / 15, bias=cst2[0:8, 18:19])
        ts(li, li, 15.0, ALU.mult)
        act(lf, psf[0:8, :], AF.Tanh, scale=1.0 / 15, bias=cst2[0:8, 19:20])
        act(lf, lf, AF.Exp, scale=-15.0)
        act(lf, lf, AF.Ln, bias=1.0)
        negsp = wr
        ts(negsp, lf, -1.0, ALU.mult)
        onesr = em
        mset(onesr, 1.0, eng="dve")
        scan(Bc, onesr, negsp, mlrow[0:8, 0:1], ALU.mult, ALU.add)
        tt(Gs, li, Bc, ALU.subtract)
        Mext = Tl(FP[0:8, 12 * 512:12 * 512 + 513], ["F12", "F13"])
        cp(Mext[:, 0:1], mlrow[0:8, 1:2], "dve")
        scan(Mext[:, 1:513], Gs, Gs, mlrow[0:8, 1:2], ALU.max, ALU.max)
        Mrow = Mext[:, 1:513]
        cp(mlrow[0:8, 0:1], Bc[:, 511:512], "dve")
        cp(mlrow[0:8, 1:2], Mext[:, 512:513], "dve")
        tt(em, Bc, Mrow, ALU.add)
        act(em, em, AF.Exp, scale=-1.0)
        M3 = Mrow.r("p (j t) -> p j t", j=4)
        Mp3 = Mext[:, 0:512].r("p (j t) -> p j t", j=4)[:, :, 0:1]
        Me3 = Mext[:, 1:513].r("p (j t) -> p j t", j=4)[:, :, 127:128]
        tt(inter.r("p (j t) -> p j t", j=4), Mp3.bc([8, 4, 128]), M3, ALU.subtract)
        act(inter, inter, AF.Exp)
        tt(wr.r("p (j t) -> p j t", j=4), Gs.r("p (j t) -> p j t", j=4), Me3.bc([8, 4, 128]), ALU.subtract)
        act(wr, wr, AF.Exp, bias=math.log(0.125))
        hl = {}
        for i_, (nm_, src_) in enumerate((("M", Mrow), ("em", em), ("in", inter))):
            hi_, lo_ = Bt(16 + i_)[0:8, 0:512], Bt(16 + i_)[0:8, 512:1024]
            cp(hi_, src_, "act")
            tt(lo_, src_, hi_, ALU.subtract)
            hl[nm_] = (hi_, lo_)
        csr = small[0:8, 32:36]
        tt(csr.r("p (j o) -> p j o", o=1), Mp3, Me3, ALU.subtract)
        act(csr, csr, AF.Exp)
        cs_hi, cs_lo = Bt(19)[0:8, 0:4], Bt(19)[0:8, 4:8]
        cp(cs_hi, csr, "act")
        tt(cs_lo, csr, cs_hi, ALU.subtract)
        proj_fm(28, range(4), lambda jc, ps: conv(jc, ps, qT[:, jc, :]))
        proj_fm(29, range(4), lambda jc, ps: conv(4 + jc, ps, kT[:, jc, :]))
        proj_tm(30, lambda j, ps: cp(vtm[:, j, 0:512], ps, "act"))
        proj_tm(31, lambda j, ps: cp(vtm[:, j, 512:1024], ps, "act"))
        proj_fm(32, range(4), lambda jc, ps: act(so[:, jc, :], ps, AF.Sigmoid))
        proj_fm(33, range(4), lambda jc, ps: act(so[:, 4 + jc, :], ps, AF.Sigmoid))
        psg = pb()
        for j in range(4):
            tp(psg[:, j * 8:(j + 1) * 8], Gs[:, j * 128:(j + 1) * 128], idf[0:8, 0:8])
            tp(psg[:, 32 + j * 8:32 + (j + 1) * 8], wr[:, j * 128:(j + 1) * 128], idf[0:8, 0:8])
        Gc = sb_gc
        ts(Gc[:, 0:32], psg[:, 0:32], math.log(0.125), ALU.add)
        cp(Gc[:, 32:64], psg[:, 32:64], "dve")
        def ml_head(h_):
            def gen(slot):
                rot, fix = slot_ps(slot)

                def pt():
                    return fix(2 + (ptc[0] % 2))
                ptc = [0]

                def nxt():
                    ptc[0] += 1
                    return fix(2 + (ptc[0] % 2))
                hp = slice((h_ % 2) * 64, (h_ % 2) * 64 + 64)
                pr = h_ // 2
                selb = cb[0:8, CB_SEL + h_ * 128: CB_SEL + (h_ + 1) * 128]
                f0 = 14 + 4 * slot
                b0 = 12 + 2 * slot
                psM = nxt()
                mm(psM, selb, hl["M"][0], st=True, sp=False)
                mm(psM, selb, hl["M"][1], st=False, sp=False)
                mm(psM, idb, mlmask, st=False, sp=True)
                E = Ft(f0)
                for j in range(4):
                    sl = slice(j * 128, (j + 1) * 128)
                    act(E[:, sl], psM[:, sl], AF.Exp, scale=-1.0, bias=Gc[:, j * 8 + h_: j * 8 + h_ + 1])
                yield
                psE = nxt()
                mm(psE, selb, hl["em"][0], st=True, sp=False)
                mm(psE, selb, hl["em"][1], st=False, sp=True)
                emt2 = Ft(f0 + 1)
                act(emt2, psE, AF.Square, scale=1e-3)
                yield
                psI = nxt()
                mm(psI, selb, hl["in"][0], st=True, sp=False)
                mm(psI, selb, hl["in"][1], st=False, sp=True)
                Qi = Bh(b0, 0)
                tt(Qi[hp, :], qT[hp, pr, :], psI[hp, :], ALU.mult)
                yield
                pcs = nxt()
                mm(pcs[:, 0:4], selb, cs_hi, st=True, sp=False)
                mm(pcs[:, 0:4], selb, cs_lo, st=False, sp=True)
                csc = small[:, 4 * slot:4 * slot + 4]
                cp(csc, pcs[:, 0:4], "dve")
                psS = nxt()
                for j in range(4):
                    sl = slice(j * 128, (j + 1) * 128)
                    mm(psS[:, sl], kT[hp, pr, sl], qT[hp, pr, sl])
                St = Bh(b0, 1)
                tt(St, psS, E, ALU.mult)
                yield
                psN, psD = fix(0), fix(1)
                kwp = kwpad[h_ % 2]
                vs = slice(h_ * 128, (h_ + 1) * 128)
                pst = nxt().bitcast(BF16)
                for j in range(4):
                    sl = slice(j * 128, (j + 1) * 128)
                    tp(pst[:, j * 64:(j + 1) * 64], kT[hp, pr, sl], idb[hp, hp])
                for j in range(4):
                    act(kwp[:, j, hp], pst[:, j * 64:(j + 1) * 64], AF.Identity, scale=Gc[:, 32 + j * 8 + h_: 32 + j * 8 + h_ + 1])
                yield
                for j in range(4):
                    sl = slice(j * 128, (j + 1) * 128)
                    mm(psN[:, sl], vtm[:, j, vs], St[:, sl], st=True, sp=False)
                    mm(psN[:, sl], mlSb[hp, pr, 0:128], Qi[hp, sl], st=False, sp=True)
                    mm(psD[:, sl], ones, St[:, sl], st=True, sp=False)
                    mm(psD[:, sl], mlSb[hp, pr, 128:256], Qi[hp, sl], st=False, sp=True)
                    psc = nxt()
                    mm(psc[:, 0:128], kwp[:, j, :], vtm[:, j, vs])
                    mm(psc[:, 128:256], kwp[:, j, :], ones)
                    stt(mlS32[hp, pr, :], mlS32[hp, pr, :], csc[hp, j:j + 1], psc[hp, 0:256], ALU.mult, ALU.add)
                    cp(mlSb[hp, pr, :], mlS32[hp, pr, :], "act")
                    yield
                nsq = Bh(b0 + 1, 0)
                act(nsq, psN, AF.Square)
                psq_ = nxt()
                mm(psq_, ones, nsq)
                den2 = Ft(f0 + 2)
                act(den2, psD, AF.Square, scale=1e-3)
                tt(den2, den2, emt2, ALU.max)
                stt(den2, psq_, 1.0 / 128, den2, ALU.mult, ALU.add)
                yield
                act(den2, den2, AF.Ln)
                act(den2, den2, AF.Exp, scale=-0.5)
                hn = Ft(f0 + 3)
                tt(hn, psN, den2, ALU.mult)
                stt(ck(yT, h_), hn, C("cnw", h_), so[:, h_, :], ALU.mult, ALU.mult)
                yield
            return gen
        run_jobs([ml_head(h_) for h_ in range(8)])

    last_out = None
    for s_ in range(nseq):
        for g_ in range(NG):
            row0 = s_ * T + g_ * G
            first = (g_ == 0)
            xtm = Ft(30, 8).r("p (j d) -> p j d", j=4)
            for kc in range(8):
                ps = pb()
                for j in range(4):
                    tp(ps[:, j * 128:(j + 1) * 128], xtm[:, j, kc * 128:(kc + 1) * 128], idf)
                if kc % 2 == 0:
                    cp(ck(xT, kc), ps, "act")
                else:
                    cp(ck(xT, kc), ps, "dve")
            rmsnorm("gmix0")
            rwkv_stage(first)
            ret_stage(first, g_ * G)
            nrow = row0 + G
            if nrow < NTOK:
                dma(xtm, x_d[nrow:nrow + G, :].rearrange("(j p) d -> p j d", p=128), "xtm", eng="pool")
            outproj([10, 11], yT)
            mlp(0, 12, 20)
            rmsnorm("gmix1")
            mlstm_stage(first)
            outproj([34, 35], yT)
            mlp(1, 36, 44)
            fin = [Ft(1 + kc) for kc in range(8)]
            rmsnorm("gfin", out_bf=False, outs=fin)
            otm = Ft(22, 8).r("p (j d) -> p j d", j=4)
            for j in range(4):
                for hh in range(2):
                    ps = pb()
                    for q_ in range(4):
                        kc = hh * 4 + q_
                        tp(ps[:, q_ * 128:(q_ + 1) * 128], fin[kc][:, j * 128:(j + 1) * 128], idf)
                    cp(otm[:, j, hh * 512:(hh + 1) * 512], ps, "act" if hh else "dve")
            last_out = dma(out_d[row0:row0 + G, :].rearrange("(j p) d -> p j d", p=128), otm, "otm", eng="pool")
    P.require("pool", last_out)
    P.emit()
    return nc


_CACHE = {}


def kernel(**inputs):
    x = np.asarray(inputs["x"], np.float32)
    B, T, _ = x.shape
    nseq = B // NCORES
    shared = _prep_shared(inputs, T)
    key = (nseq, T)
    if key not in _CACHE:
        _CACHE[key] = build(nseq, T)
    nc = _CACHE[key]
    in_maps = []
    for c in range(NCORES):
        m = dict(shared)
        m["x"] = np.ascontiguousarray(x[c * nseq:(c + 1) * nseq].reshape(nseq * T, D))
        in_maps.append(m)
    res = run_bass_kernel_spmd(nc, in_maps, core_ids=list(range(NCORES)))
    out = np.concatenate([np.asarray(r["out"], np.float32).reshape(nseq, T, D) for r in res.results], 0)
    return out
```
